# Optimizing a Trainium2 kernel written in Bass

```python
import math
import jax, jax.numpy as jnp
from jax import lax
import numpy as np


D_MODEL = 1024
BATCH = 2
SEQ = 16384
DEPTH = 4

N_A_LAYERS = DEPTH // 2
N_B_LAYERS = DEPTH - N_A_LAYERS
SSM_GROUP = 16
SSM_GROUPS = D_MODEL // SSM_GROUP
SSM_STATE = 64
SSM_CHUNK = 128
DT_MIN = 1e-3
DT_MAX = 1e-1
HEAD_DIM = 64
N_HEADS = D_MODEL // HEAD_DIM
N_KV_HEADS = N_HEADS // 4
GQA_GROUP = N_HEADS // N_KV_HEADS
WINDOW = 128
ATTN_BLOCK = 128
ROPE_THETA = 500000.0
ROT_DIM = HEAD_DIM // 4
D_FF = 4 * D_MODEL
PLE_DIM = 256
RMS_EPS = 1e-6
NEG_INF = -1e30

kernel_name = 'yoco_s5_swa_sink_hybrid'


def rmsnorm(x, g):
    xf = x.astype(jnp.float32)
    y = xf * lax.rsqrt(jnp.mean(xf * xf, axis=-1, keepdims=True) + RMS_EPS)
    return (y * g.astype(jnp.float32)).astype(x.dtype)


def partial_rope(x, pos):
    inv = ROPE_THETA ** (-jnp.arange(0, ROT_DIM, 2, dtype=jnp.float32) / ROT_DIM)
    ang = pos.astype(jnp.float32)[:, None] * inv[None, :]
    cos = jnp.cos(ang)[None, :, None, :]
    sin = jnp.sin(ang)[None, :, None, :]
    xr = x[..., :ROT_DIM].astype(jnp.float32)
    x1, x2 = xr[..., :ROT_DIM // 2], xr[..., ROT_DIM // 2:]
    rot = jnp.concatenate([x1 * cos - x2 * sin, x2 * cos + x1 * sin], axis=-1).astype(x.dtype)
    return jnp.concatenate([rot, x[..., ROT_DIM:]], axis=-1)


def s5_mixer(u, lam_re, lam_im, log_dt, b_re, b_im, c_re, c_im, d, w_glu):
    bsz, seqlen, dm = u.shape
    f32 = jnp.float32
    lam_re = lam_re.astype(f32); lam_im = lam_im.astype(f32)
    b_re = b_re.astype(f32); b_im = b_im.astype(f32)
    c_re = c_re.astype(f32); c_im = c_im.astype(f32)
    dt = jnp.exp(log_dt.astype(f32))[:, None]
    mag = jnp.exp(lam_re * dt)
    a_r = mag * jnp.cos(lam_im * dt)
    a_i = mag * jnp.sin(lam_im * dt)
    den = lam_re * lam_re + lam_im * lam_im
    nr = a_r - 1.0
    coef_r = (nr * lam_re + a_i * lam_im) / den
    coef_i = (a_i * lam_re - nr * lam_im) / den
    bb_r = coef_r[..., None] * b_re - coef_i[..., None] * b_im
    bb_i = coef_r[..., None] * b_im + coef_i[..., None] * b_re

    n_chunks = seqlen // SSM_CHUNK
    ug = u.astype(f32).reshape(bsz, n_chunks, SSM_CHUNK, SSM_GROUPS, SSM_GROUP)
    ug = jnp.transpose(ug, (1, 0, 2, 3, 4))

    def combine(e1, e2):
        a1r, a1i, s1r, s1i = e1
        a2r, a2i, s2r, s2i = e2
        return (a2r * a1r - a2i * a1i,
                a2r * a1i + a2i * a1r,
                a2r * s1r - a2i * s1i + s2r,
                a2r * s1i + a2i * s1r + s2i)

    def chunk_step(carry, uc):
        cr, ci = carry
        bur = jnp.einsum('gnh,btgh->btgn', bb_r, uc)
        bui = jnp.einsum('gnh,btgh->btgn', bb_i, uc)
        ar = jnp.broadcast_to(a_r, bur.shape)
        ai = jnp.broadcast_to(a_i, bui.shape)
        pr, pi, sr, si = lax.associative_scan(combine, (ar, ai, bur, bui), axis=1)
        xr = sr + pr * cr[:, None] - pi * ci[:, None]
        xi = si + pr * ci[:, None] + pi * cr[:, None]
        y = jnp.einsum('ghn,btgn->btgh', c_re, xr) - jnp.einsum('ghn,btgn->btgh', c_im, xi)
        return (xr[:, -1], xi[:, -1]), y

    init = (jnp.zeros((bsz, SSM_GROUPS, SSM_STATE), f32), jnp.zeros((bsz, SSM_GROUPS, SSM_STATE), f32))
    _, ys = lax.scan(chunk_step, init, ug)
    y = jnp.transpose(ys, (1, 0, 2, 3, 4)).reshape(bsz, seqlen, dm)
    y = y + d.astype(f32) * u.astype(f32)
    y = jax.nn.gelu(y).astype(u.dtype)
    ab = y @ w_glu
    a, b = ab[..., :dm], ab[..., dm:]
    return a * jax.nn.sigmoid(b)


def swa_sink_attention(q, k, v, sinks):
    bsz, seqlen = q.shape[0], q.shape[1]
    nb = seqlen // ATTN_BLOCK
    f32 = jnp.float32
    qb = q.astype(f32).reshape(bsz, nb, ATTN_BLOCK, N_KV_HEADS, GQA_GROUP, HEAD_DIM)
    kb = k.astype(f32).reshape(bsz, nb, ATTN_BLOCK, N_KV_HEADS, HEAD_DIM)
    vb = v.astype(f32).reshape(bsz, nb, ATTN_BLOCK, N_KV_HEADS, HEAD_DIM)
    k_prev = jnp.concatenate([jnp.zeros_like(kb[:, :1]), kb[:, :-1]], axis=1)
    v_prev = jnp.concatenate([jnp.zeros_like(vb[:, :1]), vb[:, :-1]], axis=1)
    kk = jnp.concatenate([k_prev, kb], axis=2)
    vv = jnp.concatenate([v_prev, vb], axis=2)
    s = jnp.einsum('bnqkgd,bnjkd->bnkgqj', qb, kk) * (HEAD_DIM ** -0.5)
    qi = jnp.arange(ATTN_BLOCK)[:, None] + ATTN_BLOCK
    kj = jnp.arange(2 * ATTN_BLOCK)[None, :]
    band = (kj <= qi) & (qi - kj < WINDOW)
    has_prev = jnp.arange(nb) > 0
    mask = band[None] & ((kj >= ATTN_BLOCK)[None] | has_prev[:, None, None])
    s = jnp.where(mask[None, :, None, None], s, NEG_INF)
    sink = sinks.astype(f32).reshape(1, 1, N_KV_HEADS, GQA_GROUP, 1, 1)
    m = jnp.maximum(jnp.max(s, axis=-1, keepdims=True), sink)
    pr = jnp.exp(s - m)
    w = pr / (jnp.sum(pr, axis=-1, keepdims=True) + jnp.exp(sink - m))
    o = jnp.einsum('bnkgqj,bnjkd->bnqkgd', w, vv)
    return o.reshape(bsz, seqlen, N_HEADS * HEAD_DIM).astype(q.dtype)


def setup_inputs(seed: int = 0) -> dict:
    key = jax.random.key(seed)
    ks = jax.random.split(key, 32)
    f32 = jnp.float32
    nrm = lambda k, shape, scale: jax.random.normal(k, shape, f32) * scale
    x = nrm(ks[0], (BATCH, SEQ, D_MODEL), 1.0)
    p = nrm(ks[1], (DEPTH, BATCH, SEQ, PLE_DIM), 1.0)
    norm_mix = 1.0 + nrm(ks[2], (DEPTH, D_MODEL), 0.02)
    ssm_lambda_re = -0.5 + nrm(ks[3], (N_A_LAYERS, SSM_GROUPS, SSM_STATE), 0.01)
    ssm_lambda_im = (jnp.pi * jnp.arange(SSM_STATE, dtype=f32))[None, None, :] + nrm(ks[4], (N_A_LAYERS, SSM_GROUPS, SSM_STATE), 0.01)
    ssm_log_dt = math.log(DT_MIN) + jax.random.uniform(ks[5], (N_A_LAYERS, SSM_GROUPS), f32) * (math.log(DT_MAX) - math.log(DT_MIN))
    ssm_b_re = nrm(ks[6], (N_A_LAYERS, SSM_GROUPS, SSM_STATE, SSM_GROUP), (2 * SSM_GROUP) ** -0.5)
    ssm_b_im = nrm(ks[7], (N_A_LAYERS, SSM_GROUPS, SSM_STATE, SSM_GROUP), (2 * SSM_GROUP) ** -0.5)
    ssm_c_re = nrm(ks[8], (N_A_LAYERS, SSM_GROUPS, SSM_GROUP, SSM_STATE), (2 * SSM_STATE) ** -0.5)
    ssm_c_im = nrm(ks[9], (N_A_LAYERS, SSM_GROUPS, SSM_GROUP, SSM_STATE), (2 * SSM_STATE) ** -0.5)
    ssm_d = nrm(ks[10], (N_A_LAYERS, D_MODEL), 1.0)
    ssm_w_glu = nrm(ks[11], (N_A_LAYERS, D_MODEL, 2 * D_MODEL), D_MODEL ** -0.5)
    kv_norm = 1.0 + nrm(ks[12], (D_MODEL,), 0.02)
    w_k = nrm(ks[13], (D_MODEL, N_KV_HEADS * HEAD_DIM), D_MODEL ** -0.5)
    w_v = nrm(ks[14], (D_MODEL, N_KV_HEADS * HEAD_DIM), D_MODEL ** -0.5)
    w_q = nrm(ks[15], (N_B_LAYERS, D_MODEL, N_HEADS * HEAD_DIM), D_MODEL ** -0.5)
    attn_sinks = nrm(ks[16], (N_B_LAYERS, N_HEADS), 0.5)
    w_o = nrm(ks[17], (N_B_LAYERS, N_HEADS * HEAD_DIM, D_MODEL), (N_HEADS * HEAD_DIM) ** -0.5)
    norm_mlp = 1.0 + nrm(ks[18], (DEPTH, D_MODEL), 0.02)
    w_up = nrm(ks[19], (DEPTH, D_MODEL, D_FF), D_MODEL ** -0.5)
    w_down = nrm(ks[20], (DEPTH, D_FF, D_MODEL), D_FF ** -0.5)
    norm_ple = 1.0 + nrm(ks[21], (DEPTH, D_MODEL), 0.02)
    w_ple_gate = nrm(ks[22], (DEPTH, D_MODEL, D_MODEL), D_MODEL ** -0.5)
    w_ple_proj = nrm(ks[23], (DEPTH, PLE_DIM, D_MODEL), PLE_DIM ** -0.5)
    norm_final = 1.0 + nrm(ks[24], (D_MODEL,), 0.02)
    return {'x': x, 'p': p, 'norm_mix': norm_mix,
            'ssm_lambda_re': ssm_lambda_re, 'ssm_lambda_im': ssm_lambda_im, 'ssm_log_dt': ssm_log_dt,
            'ssm_b_re': ssm_b_re, 'ssm_b_im': ssm_b_im, 'ssm_c_re': ssm_c_re, 'ssm_c_im': ssm_c_im,
            'ssm_d': ssm_d, 'ssm_w_glu': ssm_w_glu,
            'kv_norm': kv_norm, 'w_k': w_k, 'w_v': w_v, 'w_q': w_q, 'attn_sinks': attn_sinks, 'w_o': w_o,
            'norm_mlp': norm_mlp, 'w_up': w_up, 'w_down': w_down,
            'norm_ple': norm_ple, 'w_ple_gate': w_ple_gate, 'w_ple_proj': w_ple_proj,
            'norm_final': norm_final}


def reference(x, p, norm_mix, ssm_lambda_re, ssm_lambda_im, ssm_log_dt, ssm_b_re, ssm_b_im,
              ssm_c_re, ssm_c_im, ssm_d, ssm_w_glu, kv_norm, w_k, w_v, w_q, attn_sinks, w_o,
              norm_mlp, w_up, w_down, norm_ple, w_ple_gate, w_ple_proj, norm_final):
    bsz, seqlen, _ = x.shape
    pos = jnp.arange(seqlen, dtype=jnp.int32)
    h = x
    k_shared = None
    v_shared = None
    for i in range(DEPTH):
        hn = rmsnorm(h, norm_mix[i])
        if i < N_A_LAYERS:
            mix = s5_mixer(hn, ssm_lambda_re[i], ssm_lambda_im[i], ssm_log_dt[i], ssm_b_re[i], ssm_b_im[i],
                           ssm_c_re[i], ssm_c_im[i], ssm_d[i], ssm_w_glu[i])
        else:
            j = i - N_A_LAYERS
            q = (hn @ w_q[j]).reshape(bsz, seqlen, N_HEADS, HEAD_DIM)
            q = partial_rope(q, pos)
            mix = swa_sink_attention(q, k_shared, v_shared, attn_sinks[j]) @ w_o[j]
        h = h + mix
        hm = rmsnorm(h, norm_mlp[i])
        h = h + jnp.square(jax.nn.relu(hm @ w_up[i])) @ w_down[i]
        gate = jax.nn.sigmoid(rmsnorm(h, norm_ple[i]) @ w_ple_gate[i])
        h = h + gate * (p[i] @ w_ple_proj[i])
        if i == N_A_LAYERS - 1:
            hk = rmsnorm(h, kv_norm)
            k_shared = partial_rope((hk @ w_k).reshape(bsz, seqlen, N_KV_HEADS, HEAD_DIM), pos)
            v_shared = (hk @ w_v).reshape(bsz, seqlen, N_KV_HEADS, HEAD_DIM)
    return rmsnorm(h, norm_final)
```

```python
import math
from contextlib import ExitStack

import numpy as np
import concourse.bass as bass
import concourse.mybir as mybir
from concourse.bass_utils import run_bass_kernel_spmd

F32 = mybir.dt.float32
BF16 = mybir.dt.bfloat16
AF = mybir.ActivationFunctionType
ALU = mybir.AluOpType
AX = mybir.AxisListType

D = 1024
KD = 8
DFF = 4096
PLE = 256
NH = 16
NKV = 4
HD = 64
T = 8
NPAIR = 32
EPS = 1e-6
ROPE_THETA = 500000.0
ROT = 16

SAME_ENGINE_SYNC = True


class Sched:
    ENG = ("pe", "act", "dve", "pool", "sp")

    def __init__(self, nc, stack):
        self.nc = nc
        self.stack = stack
        self.prog = {e: [] for e in self.ENG}
        self.sem = {}
        self.count = {}
        self.unit = {}
        self.last_write = {}
        self.readers = {}
        self.waited = {e: {} for e in self.ENG}
        for e in self.ENG:
            self._stream(e, 1)

    def _stream(self, name, unit):
        if name not in self.sem:
            self.sem[name] = self.stack.enter_context(self.nc.semaphore("s_" + name.replace(":", "_")))
            self.count[name] = 0
            self.unit[name] = unit
        return name

    def _wait(self, engine, stream, value):
        if self.waited[engine].get(stream, 0) >= value:
            return
        self.waited[engine][stream] = value
        sem = self.sem[stream]
        self.prog[engine].append(lambda eng, sem=sem, value=value: eng.wait_ge(sem, value))

    def add(self, engine, emit, reads=(), writes=(), chan=None):
        stream = engine if chan is None else self._stream("dma:" + chan, 16)
        deps = {}
        for k in reads:
            lw = self.last_write.get(k)
            if lw is not None:
                deps[lw[0]] = max(deps.get(lw[0], 0), lw[1] + 1)
        for k in writes:
            lw = self.last_write.get(k)
            if lw is not None:
                deps[lw[0]] = max(deps.get(lw[0], 0), lw[1] + 1)
            for r in self.readers.get(k, ()):
                deps[r[0]] = max(deps.get(r[0], 0), r[1] + 1)
        if chan is not None and self.count[stream] > 0:
            deps[stream] = max(deps.get(stream, 0), self.count[stream])
        for s, n in deps.items():
            if s == engine and chan is None:
                if engine == "pe" or not SAME_ENGINE_SYNC:
                    continue
            self._wait(engine, s, n * self.unit[s])
        sem = self.sem[stream]
        unit = self.unit[stream]
        ops = [emit] if isinstance(emit, tuple) else list(emit)
        assert ops and all(isinstance(o, tuple) for o in ops)

        def run(eng, ops=ops, sem=sem, unit=unit):
            last = None
            for (meth, args, kw) in ops:
                last = getattr(eng, meth)(*args, **kw)
            last.then_inc(sem, unit)
        self.prog[engine].append(run)
        idx = self.count[stream]
        self.count[stream] = idx + 1
        for k in writes:
            self.last_write[k] = (stream, idx)
            self.readers[k] = []
        for k in reads:
            self.readers.setdefault(k, []).append((stream, idx))

    def barrier(self):
        for e in self.ENG:
            for s in self.sem:
                if s != e and self.count[s] > 0:
                    self._wait(e, s, self.count[s] * self.unit[s])
        self.last_write.clear()
        self.readers.clear()

    def finish(self):
        self.barrier()
        nc = self.nc
        with nc.Block() as block:
            @block.tensor
            def _(eng):
                for f in self.prog["pe"]:
                    f(eng)

            @block.scalar
            def _(eng):
                for f in self.prog["act"]:
                    f(eng)

            @block.vector
            def _(eng):
                for f in self.prog["dve"]:
                    f(eng)

            @block.gpsimd
            def _(eng):
                for f in self.prog["pool"]:
                    f(eng)

            @block.sync
            def _(eng):
                for f in self.prog["sp"]:
                    f(eng)


def I(meth, *args, **kw):
    return (meth, args, kw)


class Arena:
    def __init__(self, ap, words):
        self.ap = ap
        self.words = words
        self.off = 0

    def reset(self, to=0):
        self.off = to

    def alloc(self, free_shape, dtype):
        n = 1
        for s in free_shape:
            n *= s
        w = n if dtype == F32 else (n + 1) // 2
        w = (w + 7) // 8 * 8
        assert self.off + w <= self.words, f"SBUF arena overflow {self.off}+{w}>{self.words}"
        v = self.ap[:, self.off:self.off + w]
        self.off += w
        if dtype != F32:
            v = v.bitcast(dtype)
        v = v[:, 0:n]
        if len(free_shape) == 1:
            return v
        names = " ".join(f"a{i}" for i in range(len(free_shape)))
        kw = {f"a{i}": s for i, s in enumerate(free_shape[:-1])}
        return v.rearrange(f"p ({names}) -> p {names}", **kw)


def tiles_of(start, total, size):
    out = []
    t = start
    end = start + total
    while t < end:
        n = min(size, end - t)
        out.append((t, n))
        t += n
    return out


def build_program(WIN, OWN, HALO=128, ARENA_WORDS=48 * 1024):
    OWNH = OWN + HALO
    PRE = WIN - OWNH
    assert PRE >= 0 and PRE % 128 == 0 and OWN % 512 == 0
    NBLK = OWNH // 128
    nc = bass.Bass("TRN2", target_bir_lowering=False)
    dt_in = lambda name, shape: nc.dram_tensor(name, list(shape), F32, kind="ExternalInput").ap()
    xT = dt_in("xT", [D, WIN])
    p0T = dt_in("p0T", [PLE, WIN])
    pLT = dt_in("pLT", [3, PLE, OWNH])
    cosT = dt_in("cosT", [128, OWNH])
    sinT = dt_in("sinT", [128, OWNH])
    mask0 = dt_in("mask0", [128, 256])
    maskG = dt_in("maskG", [128, 256])
    ident_d = dt_in("ident", [128, 128])
    bdmask_d = dt_in("bdmask", [128, 128])
    norm_mix = dt_in("norm_mix", [4, D])
    lam_re = dt_in("ssm_lambda_re", [2, 64, 64])
    lam_im = dt_in("ssm_lambda_im", [2, 64, 64])
    log_dt = dt_in("ssm_log_dt", [2, 64])
    b_re = dt_in("ssm_b_re", [2, 64, 64, 16])
    b_im = dt_in("ssm_b_im", [2, 64, 64, 16])
    c_re = dt_in("ssm_c_re", [2, 64, 16, 64])
    c_im = dt_in("ssm_c_im", [2, 64, 16, 64])
    ssm_d = dt_in("ssm_d", [2, D])
    w_glu = dt_in("ssm_w_glu", [2, D, 2 * D])
    kv_norm = dt_in("kv_norm", [D])
    w_k = dt_in("w_k", [D, 256])
    w_ks = dt_in("w_ks", [D, 256])
    w_v = dt_in("w_v", [D, 256])
    w_q = dt_in("w_q", [2, D, D])
    w_qs = dt_in("w_qs", [2, D, D])
    sinks = dt_in("attn_sinks", [2, NH])
    w_o = dt_in("w_o", [2, D, D])
    norm_mlp = dt_in("norm_mlp", [4, D])
    w_up = dt_in("w_up", [4, D, DFF])
    w_down = dt_in("w_down", [4, DFF, D])
    norm_ple = dt_in("norm_ple", [4, D])
    w_gate = dt_in("w_ple_gate", [4, D, D])
    w_proj = dt_in("w_ple_proj", [4, PLE, D])
    norm_final = dt_in("norm_final", [D])
    outT = nc.dram_tensor("outT", [D, OWN], F32, kind="ExternalOutput").ap()
    hA = nc.dram_tensor("hA", [D, WIN], F32, kind="Internal").ap()
    hB = nc.dram_tensor("hB", [D, OWNH], F32, kind="Internal").ap()
    kT2_d = nc.dram_tensor("kT2_d", [128, NKV, OWNH], BF16, kind="Internal").ap()
    dbg_d = nc.dram_tensor("dbg", [128, DBGW], F32, kind="ExternalOutput").ap() if DBGW else None
    dbg_off = [0]
    dbg_map = {}
    v_d = nc.dram_tensor("v_d", [128, NBLK, 256], BF16, kind="Internal").ap()

    stack = ExitStack()
    with stack:
        S = Sched(nc, stack)
        arena_t = stack.enter_context(nc.sbuf_tensor("arena", [128, ARENA_WORDS], F32))
        A = Arena(arena_t[:], ARENA_WORDS)
        psb = [stack.enter_context(nc.psum_tensor(f"ps{i}", [128, 512], F32)) for i in range(8)]
        ps_pool = [list(range(8))]
        ps_rr = [0]

        def next_ps():
            pool = ps_pool[0]
            i = pool[ps_rr[0] % len(pool)]
            ps_rr[0] += 1
            return i

        ident = A.alloc([128], F32)
        ident_bf = A.alloc([128], BF16)
        bdmask = A.alloc([128], F32)
        ones_bf = A.alloc([128], BF16)
        gam = A.alloc([14, KD], F32)
        dvec = A.alloc([2, KD], F32)
        sink_t = A.alloc([2, NH], F32)
        mask0_t = A.alloc([256], F32)
        maskG_t = A.alloc([256], F32)
        xcar = A.alloc([2, NPAIR], F32)
        eps_t = A.alloc([1], F32)
        hpi_t = A.alloc([1], F32)
        PERSIST = A.off

        def dump(name, ap, keys):
            if dbg_d is None or name in dbg_map:
                return
            w = 1
            for d_ in ap.shape[1:]:
                w *= d_
            if dbg_off[0] + w > DBGW:
                return
            dst = dbg_d[:, dbg_off[0]:dbg_off[0] + w]
            if len(ap.shape) > 2:
                names = " ".join(f"a{i}" for i in range(len(ap.shape) - 1))
                kw = {f"a{i}": d_ for i, d_ in enumerate(ap.shape[1:-1])}
                dst = dst.rearrange(f"p ({names}) -> p {names}", **kw)
            dbg_map[name] = (dbg_off[0], tuple(ap.shape[1:]))
            dbg_off[0] += w
            S.add("sp" if ap.dtype == F32 else "pool", I("dma_start", out=dst, in_=ap), reads=tuple(keys),
                  writes=(("dbg", name),), chan="dbgc" if ap.dtype == F32 else "dbgp")

        def dma_in(eng, out, in_, key, chan, **kw):
            S.add(eng, I("dma_start", out=out, in_=in_, **kw), reads=(), writes=(key,), chan=chan)

        dma_in("sp", ident, ident_d, "ident", "c0")
        dma_in("sp", bdmask, bdmask_d, "bdmask", "c0")
        dma_in("sp", mask0_t, mask0, "mask0", "c0")
        dma_in("sp", maskG_t, maskG, "maskG", "c0")
        gsrc = [norm_mix[i] for i in range(4)] + [norm_mlp[i] for i in range(4)] + \
               [norm_ple[i] for i in range(4)] + [kv_norm, norm_final]
        for i, g in enumerate(gsrc):
            dma_in("sp", gam[:, i, :], g.rearrange("(k p) -> p k", p=128), "gam", "c0",
                   allow_slow_non_contiguous=True)
        for i in range(2):
            dma_in("sp", dvec[:, i, :], ssm_d[i].rearrange("(k p) -> p k", p=128), "dvec", "c0",
                   allow_slow_non_contiguous=True)
            dma_in("sp", sink_t[:, i, :], sinks[i].partition_broadcast(128), "sink", "c0")
        S.add("dve", I("memset", ones_bf, 1.0 / D), writes=("ones",))
        S.add("dve", I("tensor_copy", ident_bf, ident), reads=("ident",), writes=("identbf",))
        S.add("dve", I("memset", xcar, 0.0), writes=("xcar",))
        S.add("dve", I("memset", eps_t, EPS), writes=("eps",))
        S.add("dve", I("memset", hpi_t, math.pi / 2), writes=("hpi",))
        G_MIX, G_MLP, G_PLE, G_KV, G_FIN = 0, 4, 8, 12, 13

        def load_w(dst, src_ap, key, chan="w"):
            S.add("pool", I("dma_start", out=dst, in_=src_ap), writes=(key,), chan=chan)

        def rmsnorm(h_t, hkey, gidx, okey, n, sq_t, rstd_t, out_of, view=None):
            vw = (lambda a: a) if view is None else view
            for k in range(KD):
                S.add("act", I("activation", sq_t[:, k, 0:n], h_t[:, k, 0:n], AF.Square),
                      reads=(hkey,), writes=(("sq", k),))
            b = next_ps()
            ps = psb[b][:, 0:n]
            S.add("pe", [I("matmul", ps, ones_bf, sq_t[:, k, 0:n], start=(k == 0), stop=(k == KD - 1))
                         for k in range(KD)],
                  reads=[("sq", k) for k in range(KD)] + ["ones"], writes=(("ps", b),))
            S.add("act", I("activation", rstd_t[:, 0:n], ps, AF.Sqrt, bias=eps_t, scale=1.0),
                  reads=(("ps", b), "eps"), writes=("rstd",))
            S.add("dve", I("reciprocal", rstd_t[:, 0:n], rstd_t[:, 0:n]), reads=("rstd",), writes=("rstd",))
            for k in range(KD):
                S.add("dve", I("scalar_tensor_tensor", out_of(k), vw(h_t[:, k, 0:n]), gam[:, gidx, k:k + 1],
                               vw(rstd_t[:, 0:n]), ALU.mult, ALU.mult),
                      reads=(hkey, "rstd", "gam"), writes=((okey, k),))

        def gemm(w_t, wkey, kin, act_of, act_keys, mlist, n, evac):
            for m in mlist:
                b = next_ps()
                ps = psb[b][:, 0:n]
                S.add("pe", [I("matmul", ps, w_t[:, k, m * 128:(m + 1) * 128], act_of(k),
                               start=(k == 0), stop=(k == kin - 1)) for k in range(kin)],
                      reads=[wkey] + list(act_keys), writes=(("ps", b),))
                evac(m, b, ps)

        def tile_loader(src_ap, bufs, keyname, chan, src_key, nslots=2):
            def load(i, t0, n, col0):
                hb = bufs[i % nslots]
                S.add("sp", I("dma_start", out=hb[:, :, 0:n],
                              in_=src_ap[:, col0:col0 + n].rearrange("(k p) n -> p k n", p=128)),
                      reads=((src_key, t0),), writes=((keyname, i % nslots),), chan=f"{chan}{i % nslots}")
            return load

        def s5_prep(li, SIN, OUT, KTt, alpha):
            base = A.off
            lr = A.alloc([NPAIR], F32); lim = A.alloc([NPAIR], F32); dtt = A.alloc([NPAIR], F32)
            t1 = A.alloc([NPAIR], F32); t2 = A.alloc([NPAIR], F32); t3 = A.alloc([NPAIR], F32)
            akr = A.alloc([T + 1, NPAIR], F32); aki = A.alloc([T + 1, NPAIR], F32)
            cfr = A.alloc([NPAIR], F32); cfi = A.alloc([NPAIR], F32)
            bR = A.alloc([NPAIR, 16], F32); bI = A.alloc([NPAIR, 16], F32)
            bbr = A.alloc([NPAIR, 16], F32); bbi = A.alloc([NPAIR, 16], F32)
            zr = A.alloc([NPAIR, 16], F32); zi = A.alloc([NPAIR, 16], F32)
            tb1 = A.alloc([NPAIR, 16], F32); tb2 = A.alloc([NPAIR, 16], F32)
            cnat = A.alloc([2, KD, 128], F32)
            cR = A.alloc([NPAIR, 16], F32); cI = A.alloc([NPAIR, 16], F32)
            zpr = A.alloc([128], F32); zpi = A.alloc([128], F32)
            cpr = A.alloc([KD, 128], F32); cpi = A.alloc([KD, 128], F32)
            PK = "prep"

            def l1(src):
                return src.rearrange("(q e) n -> e n q", e=2)
            for e_ in range(2):
                sl = slice(64 * e_, 64 * e_ + 64)
                dma_in("sp", lr[sl, :], l1(lam_re[li])[e_], PK, "c0", allow_slow_non_contiguous=True)
                dma_in("sp", lim[sl, :], l1(lam_im[li])[e_], PK, "c0", allow_slow_non_contiguous=True)
                dma_in("sp", dtt[sl, :], log_dt[li].rearrange("(q e) -> e q", e=2)[e_].partition_broadcast(64),
                       PK, "c0", allow_slow_non_contiguous=True)
                dma_in("sp", bR[sl, :, :], b_re[li].rearrange("(q e) n h -> e n q h", e=2)[e_], PK, "c0",
                       allow_slow_non_contiguous=True)
                dma_in("sp", bI[sl, :, :], b_im[li].rearrange("(q e) n h -> e n q h", e=2)[e_], PK, "c0",
                       allow_slow_non_contiguous=True)
            for r, csrc in enumerate((c_re, c_im)):
                for dup in range(2):
                    dma_in("sp", cnat[:, r, :, 64 * dup:64 * dup + 64],
                           csrc[li].rearrange("(k g) h n -> (g h) k n", k=KD), PK, "c0")

            def V(eng, inst, r=(PK,), w=(PK,)):
                S.add(eng, inst, reads=r, writes=w)

            def TT(o, a, b, op):
                V("dve", I("tensor_tensor", o, a, b, op))
            def exp_small(o, z):
                V("dve", I("tensor_scalar", o, z, 1.0 / 6, 1.0, ALU.mult, ALU.add))
                for dv in (5.0, 4.0, 3.0, 2.0, 1.0):
                    TT(o, o, z, ALU.mult)
                    V("dve", I("tensor_scalar", o, o, 1.0 / dv, 1.0, ALU.mult, ALU.add))
            V("dve", I("tensor_scalar_mul", t3, dtt, 1.0 / 64))
            exp_small(dtt, t3)
            for _ in range(6):
                TT(dtt, dtt, dtt, ALU.mult)
            TT(t3, lr, dtt, ALU.mult)
            exp_small(t1, t3)
            TT(t2, lim, dtt, ALU.mult)
            V("act", I("activation", aki[:, 1, :], t2, AF.Sin, scale=1.0 / 16), r=(PK, "hpi"))
            V("act", I("activation", akr[:, 1, :], t2, AF.Sin, bias=hpi_t, scale=1.0 / 16), r=(PK, "hpi"))
            for _ in range(4):
                TT(t3, akr[:, 1, :], aki[:, 1, :], ALU.mult)
                TT(akr[:, 1, :], akr[:, 1, :], akr[:, 1, :], ALU.mult)
                TT(aki[:, 1, :], aki[:, 1, :], aki[:, 1, :], ALU.mult)
                TT(akr[:, 1, :], akr[:, 1, :], aki[:, 1, :], ALU.subtract)
                V("dve", I("tensor_scalar_mul", aki[:, 1, :], t3, 2.0))
            TT(t3, akr[:, 1, :], akr[:, 1, :], ALU.mult)
            TT(t2, aki[:, 1, :], aki[:, 1, :], ALU.mult)
            TT(t3, t3, t2, ALU.add)
            V("dve", I("tensor_scalar", t3, t3, -0.5, 1.5, ALU.mult, ALU.add))
            TT(t1, t1, t3, ALU.mult)
            TT(akr[:, 1, :], akr[:, 1, :], t1, ALU.mult)
            TT(aki[:, 1, :], aki[:, 1, :], t1, ALU.mult)
            V("dve", I("memset", akr[:, 0, :], 1.0))
            V("dve", I("memset", aki[:, 0, :], 0.0))
            for k in range(2, T + 1):
                TT(t1, akr[:, k - 1, :], akr[:, 1, :], ALU.mult)
                TT(t2, aki[:, k - 1, :], aki[:, 1, :], ALU.mult)
                TT(akr[:, k, :], t1, t2, ALU.subtract)
                TT(t1, akr[:, k - 1, :], aki[:, 1, :], ALU.mult)
                TT(t2, aki[:, k - 1, :], akr[:, 1, :], ALU.mult)
                TT(aki[:, k, :], t1, t2, ALU.add)
            dump("dtt", dtt, [PK]); dump("akr", akr, [PK]); dump("aki", aki, [PK])
            V("dve", I("tensor_copy", alpha[:, 0, :], akr[:, T, :]), w=(PK, "alpha"))
            V("dve", I("tensor_copy", alpha[:, 1, :], aki[:, T, :]), w=(PK, "alpha"))
            V("dve", I("tensor_scalar_add", t1, akr[:, 1, :], -1.0))
            TT(t2, lr, lr, ALU.mult)
            TT(t3, lim, lim, ALU.mult)
            TT(t2, t2, t3, ALU.add)
            V("dve", I("reciprocal", t2, t2))
            TT(cfr, t1, lr, ALU.mult)
            TT(t3, aki[:, 1, :], lim, ALU.mult)
            TT(cfr, cfr, t3, ALU.add)
            TT(cfr, cfr, t2, ALU.mult)
            TT(cfi, aki[:, 1, :], lr, ALU.mult)
            TT(t3, t1, lim, ALU.mult)
            TT(cfi, cfi, t3, ALU.subtract)
            TT(cfi, cfi, t2, ALU.mult)

            def bc(v):
                return v.unsqueeze(2).to_broadcast([128, NPAIR, 16])

            def cmul(outr, outi, sr, si, vr, vi):
                TT(tb1, vr, bc(sr), ALU.mult)
                TT(tb2, vi, bc(si), ALU.mult)
                TT(outr, tb1, tb2, ALU.subtract)
                TT(tb1, vi, bc(sr), ALU.mult)
                TT(tb2, vr, bc(si), ALU.mult)
                TT(outi, tb1, tb2, ALU.add)
            cmul(bbr, bbi, cfr, cfi, bR, bI)
            dump("cfr", cfr, [PK]); dump("cfi", cfi, [PK]); dump("bbr", bbr, [PK]); dump("bbi", bbi, [PK])

            for r, dstc in enumerate((cR, cI)):
                for k in range(KD):
                    b = next_ps()
                    ps = psb[b][:, 0:128]
                    S.add("pe", I("transpose", ps, cnat[:, r, k, :], ident),
                          reads=(PK, "ident"), writes=(("ps", b),))
                    psv = ps.rearrange("p (q e h) -> p q e h", q=4, e=2)
                    S.add("act", I("activation", dstc[0:64, 4 * k:4 * k + 4, :], psv[0:64, :, 0, :], AF.Copy),
                          reads=(("ps", b),), writes=(PK,))
                    S.add("act", I("activation", dstc[64:128, 4 * k:4 * k + 4, :], psv[64:128, :, 1, :], AF.Copy),
                          reads=(("ps", b),), writes=(PK,))
            dump("cR", cR, [PK]); dump("cI", cI, [PK])
            S.add("pool", I("memset", OUT, 0.0), writes=("OUT",))
            for j in range(T):
                cmul(zr, zi, akr[:, j + 1, :], aki[:, j + 1, :], cR, cI)
                for e_ in range(2):
                    sl = slice(64 * e_, 64 * e_ + 64)
                    S.add("act", I("activation", OUT[sl, :, j, 0, 16 * e_:16 * e_ + 16], zr[sl, :, :], AF.Copy),
                          reads=(PK,), writes=("OUT",))
                    S.add("act", I("activation", OUT[sl, :, j, 1, 16 * e_:16 * e_ + 16], zi[sl, :, :], AF.Copy,
                                   scale=-1.0), reads=(PK,), writes=("OUT",))
            V("dve", I("memset", cpr, 0.0))
            V("dve", I("memset", cpi, 0.0))
            for k in range(KD):
                for e_ in range(2):
                    sl = slice(64 * e_, 64 * e_ + 64)
                    dv_r = cpr[:, k, :].rearrange("p (q e h) -> p q e h", q=4, e=2)
                    dv_i = cpi[:, k, :].rearrange("p (q e h) -> p q e h", q=4, e=2)
                    V("dve", I("tensor_copy", dv_r[sl, :, e_, :], cR[sl, 4 * k:4 * k + 4, :]))
                    V("dve", I("tensor_scalar_mul", dv_i[sl, :, e_, :], cI[sl, 4 * k:4 * k + 4, :], -1.0))
            V("dve", I("memset", zpr, 0.0))
            V("dve", I("memset", zpi, 0.0))
            zvr = zpr.rearrange("p (q e h) -> p q e h", q=4, e=2)
            zvi = zpi.rearrange("p (q e h) -> p q e h", q=4, e=2)
            for kk in range(T):
                cmul(zr, zi, akr[:, kk, :], aki[:, kk, :], bbr, bbi)
                s_ = T - 1 - kk
                for k in range(KD):
                    for e_ in range(2):
                        sl = slice(64 * e_, 64 * e_ + 64)
                        V("dve", I("tensor_copy", zvr[sl, :, e_, :], zr[sl, 4 * k:4 * k + 4, :]))
                        V("dve", I("tensor_copy", zvi[sl, :, e_, :], zi[sl, 4 * k:4 * k + 4, :]))
                    for part, zp in enumerate((zpr, zpi)):
                        b = next_ps()
                        ps = psb[b][:, 0:128]
                        S.add("pe", I("transpose", ps, zp, ident), reads=(PK, "ident"), writes=(("ps", b),))
                        S.add("act", I("activation", SIN[:, k, s_, part, :], ps, AF.Copy),
                              reads=(("ps", b),), writes=("SIN",))
                    b = next_ps()
                    ps = psb[b][:, 0:128]
                    S.add("pe", [I("matmul", ps, zpr, cpr[:, k, :], start=True, stop=False),
                                 I("matmul", ps, zpi, cpi[:, k, :], start=False, stop=True)],
                          reads=(PK,), writes=(("ps", b),))
                    S.add("dve", I("tensor_tensor", KTt[:, k, kk, :], ps, bdmask, ALU.mult),
                          reads=(("ps", b), "bdmask"), writes=("KT", PK))
            A.reset(base)

        def s5_phase(li, src, dst, out_start):
            A.reset(PERSIST)
            ps_pool[0] = list(range(8))
            SIN = A.alloc([KD, T, 2, 128], BF16)
            OUT = A.alloc([NPAIR, T, 2, 32], BF16)
            KTt = A.alloc([KD, T, 128], BF16)
            alpha = A.alloc([2, NPAIR], F32)
            wg = A.alloc([KD, 2 * D], BF16)
            load_w(wg, w_glu[li].rearrange("(k p) o -> p k o", p=128), "wglu")
            s5_prep(li, SIN, OUT, KTt, alpha)
            S.barrier()
            NMAX = 256
            CM = NMAX // T
            hbuf = [A.alloc([KD, NMAX], F32) for _ in range(3)]
            sq_t = A.alloc([KD, NMAX], BF16)
            rstd_t = A.alloc([NMAX], F32)
            u_bufs = [A.alloc([KD, NMAX], BF16) for _ in range(2)]
            S_bufs = [A.alloc([2, NPAIR, CM], F32) for _ in range(2)]
            X = A.alloc([2, NPAIR, CM + 1], F32)
            X_bf = A.alloc([2, NPAIR, CM], BF16)
            tA = A.alloc([NPAIR], F32); tB = A.alloc([NPAIR], F32)
            tC = A.alloc([NPAIR], F32); tD = A.alloc([NPAIR], F32)
            v_t = A.alloc([NMAX], F32); w_t2 = A.alloc([NMAX], F32); sg_t = A.alloc([NMAX], F32)
            g_bf = A.alloc([KD, NMAX], BF16)
            S.add("dve", I("memset", xcar, 0.0), writes=("xcar",))
            rem = (WIN - out_start) % NMAX
            tl = tiles_of(0, out_start, NMAX) + tiles_of(out_start, rem, NMAX) + \
                tiles_of(out_start + rem, WIN - out_start - rem, NMAX)
            load = tile_loader(src, hbuf, "h", "h", "src", nslots=3)
            SK = ("S", "X", "xcar", "alpha")
            ar, ai = alpha[:, 0, :], alpha[:, 1, :]

            def stage1(i, t0, n):
                hb = hbuf[i % 3]
                hkey = ("h", i % 3)
                u_bf = u_bufs[i % 2]
                Ssb = S_bufs[i % 2]
                UK = "u%d" % (i % 2)
                SKEY = "S%d" % (i % 2)
                C = n // T
                pv = lambda a: a.rearrange("p (c s) -> p c s", s=T)
                rmsnorm(hb, hkey, G_MIX + li, UK, n, sq_t, rstd_t,
                        out_of=lambda k: u_bf[:, k, 0:n].rearrange("p (s c) -> p c s", s=T), view=pv)
                for q in range(NPAIR):
                    k, r0 = q // 4, 32 * (q % 4)
                    for part in range(2):
                        b = next_ps()
                        ps = psb[b][:, 0:C]
                        S.add("pe", [I("matmul", ps, SIN[r0:r0 + 32, k, s_, part, :],
                                       u_bf[r0:r0 + 32, k, s_ * C:(s_ + 1) * C],
                                       start=(s_ == 0), stop=(s_ == T - 1), tile_position=(r0, 0))
                                     for s_ in range(T)],
                              reads=("SIN", (UK, k)), writes=(("ps", b),))
                        S.add("act", I("activation", Ssb[:, part, q, 0:C], ps, AF.Copy),
                              reads=(("ps", b),), writes=(SKEY,))

            def stage2(i, t0, n):
                hb = hbuf[i % 3]
                hkey = ("h", i % 3)
                u_bf = u_bufs[i % 2]
                Ssb = S_bufs[i % 2]
                UK = "u%d" % (i % 2)
                SKEY = "S%d" % (i % 2)
                SK = (SKEY, "X", "xcar", "alpha")
                C = n // T
                full = t0 >= out_start
                S.add("dve", I("tensor_copy", X[:, :, :, 0], xcar), reads=SK, writes=("X",))
                for c in range(C):
                    xr, xi = X[:, 0, :, c], X[:, 1, :, c]
                    S.add("dve", I("tensor_tensor", tA, xr, ar, ALU.mult), reads=("X", "alpha", "xcar"), writes=("tA",))
                    S.add("dve", I("tensor_tensor", tB, xi, ai, ALU.mult), reads=("X", "alpha"), writes=("tB",))
                    S.add("dve", I("tensor_tensor", tC, xi, ar, ALU.mult), reads=("X", "alpha"), writes=("tC",))
                    S.add("dve", I("tensor_tensor", tD, xr, ai, ALU.mult), reads=("X", "alpha"), writes=("tD",))
                    S.add("dve", I("tensor_tensor", tA, tA, tB, ALU.subtract), reads=("tA", "tB"), writes=("tA",))
                    S.add("dve", I("tensor_tensor", tC, tC, tD, ALU.add), reads=("tC", "tD"), writes=("tC",))
                    S.add("dve", I("tensor_tensor", X[:, 1, :, c + 1], tC, Ssb[:, 1, :, c], ALU.add),
                          reads=("tC", SKEY), writes=("X",))
                    S.add("dve", I("tensor_tensor", X[:, 0, :, c + 1], tA, Ssb[:, 0, :, c], ALU.add),
                          reads=("tA", SKEY), writes=("X",))
                S.add("dve", I("tensor_copy", xcar, X[:, :, :, C]), reads=SK, writes=("xcar",))
                if not full:
                    return
                S.add("act", I("activation", X_bf[:, :, :, 0:C], X[:, :, :, 0:C], AF.Copy),
                      reads=("X",), writes=("Xbf",))
                for k in range(KD):
                    b = next_ps()
                    ps = psb[b][:, 0:n]
                    ops = []
                    for j in range(T):
                        o = ps[:, j * C:(j + 1) * C]
                        for r in range(j + 1):
                            ops.append(I("matmul", o, KTt[:, k, j - r, :], u_bf[:, k, r * C:(r + 1) * C],
                                         start=(r == 0), stop=False))
                        for qq in range(4):
                            q = 4 * k + qq
                            for part in range(2):
                                ops.append(I("matmul", ps[32 * qq:32 * qq + 32, j * C:(j + 1) * C],
                                             OUT[:, q, j, part, :], X_bf[:, part, q, 0:C],
                                             start=False, stop=(part == 1),
                                             tile_position=(0, 32 * qq)))
                    S.add("pe", ops, reads=("KT", "OUT", "Xbf", (UK, k)), writes=(("ps", b),))
                    S.add("dve", I("scalar_tensor_tensor", v_t[:, 0:n], u_bf[:, k, 0:n], dvec[:, li, k:k + 1], ps,
                                   ALU.mult, ALU.add), reads=(("ps", b), (UK, k), "dvec"), writes=("v",))
                    S.add("act", I("activation", w_t2[:, 0:n], v_t[:, 0:n], AF.Square), reads=("v",), writes=("w2",))
                    S.add("dve", I("tensor_scalar", w_t2[:, 0:n], w_t2[:, 0:n], 0.044715, 1.0, ALU.mult, ALU.add),
                          reads=("w2",), writes=("w2",))
                    S.add("dve", I("tensor_tensor", w_t2[:, 0:n], w_t2[:, 0:n], v_t[:, 0:n], ALU.mult),
                          reads=("w2", "v"), writes=("w2",))
                    S.add("act", I("activation", sg_t[:, 0:n], w_t2[:, 0:n], AF.Sigmoid, scale=1.5957691216057308),
                          reads=("w2",), writes=("sg",))
                    S.add("dve", I("tensor_tensor", g_bf[:, k, 0:n], v_t[:, 0:n], sg_t[:, 0:n], ALU.mult),
                          reads=("v", "sg"), writes=(("g", k),))
                gkeys = [("g", k) for k in range(KD)]
                for m in range(KD):
                    ba = next_ps(); bb_ = next_ps()
                    psa = psb[ba][:, 0:n]; psg = psb[bb_][:, 0:n]
                    S.add("pe", [I("matmul", psa, wg[:, k, m * 128:(m + 1) * 128], g_bf[:, k, 0:n],
                                   start=(k == 0), stop=(k == KD - 1)) for k in range(KD)] +
                                [I("matmul", psg, wg[:, k, D + m * 128:D + (m + 1) * 128], g_bf[:, k, 0:n],
                                   start=(k == 0), stop=(k == KD - 1)) for k in range(KD)],
                          reads=["wglu"] + gkeys, writes=(("ps", ba), ("ps", bb_)))
                    S.add("act", I("activation", sg_t[:, 0:n], psg, AF.Sigmoid), reads=(("ps", bb_),), writes=("sg",))
                    S.add("dve", I("tensor_tensor", v_t[:, 0:n], psa, sg_t[:, 0:n], ALU.mult),
                          reads=(("ps", ba), "sg"), writes=("v",))
                    hv = hb[:, m, 0:n].rearrange("p (c s) -> p s c", s=T)
                    S.add("dve", I("tensor_tensor", hv, hv, v_t[:, 0:n].rearrange("p (s c) -> p s c", s=T), ALU.add),
                          reads=("v", hkey), writes=(hkey,))
                c0 = t0 - out_start
                S.add("sp", I("dma_start", out=dst[:, c0:c0 + n].rearrange("(k p) n -> p k n", p=128),
                              in_=hb[:, :, 0:n]), reads=(hkey,), writes=(("dst", t0),), chan=f"st{i % 3}")

            for i0 in range(min(2, len(tl))):
                load(i0, tl[i0][0], tl[i0][1], tl[i0][0])
            stage1(0, tl[0][0], tl[0][1])
            for i, (t0, n) in enumerate(tl):
                if i + 2 < len(tl):
                    load(i + 2, tl[i + 2][0], tl[i + 2][1], tl[i + 2][0])
                if i + 1 < len(tl):
                    stage1(i + 1, tl[i + 1][0], tl[i + 1][1])
                stage2(i, t0, n)
            S.barrier()

        def mlp_phase(layer, buf, start, total):
            A.reset(PERSIST)
            ps_pool[0] = list(range(8))
            wu = A.alloc([KD, DFF], BF16)
            wd = A.alloc([DFF // 128, D], BF16)
            for k in range(KD):
                load_w(wu[:, k, :], w_up[layer][k * 128:(k + 1) * 128, :], "wu")
            for k4 in range(0, DFF // 128, 8):
                load_w(wd[:, k4:k4 + 8, :],
                       w_down[layer][k4 * 128:(k4 + 8) * 128, :].rearrange("(k p) o -> p k o", p=128), "wd")
            NM = 256
            hbuf = [A.alloc([KD, NM], F32) for _ in range(2)]
            sq_t = A.alloc([KD, NM], BF16)
            rstd_t = A.alloc([NM], F32)
            hm = A.alloc([KD, NM], BF16)
            ff = A.alloc([DFF // 128, NM], BF16)
            r_t = [A.alloc([NM], F32) for _ in range(2)]
            tl = tiles_of(start, total % NM, NM) + tiles_of(start + total % NM, total - total % NM, NM)
            load = tile_loader(buf, hbuf, "h", "h", "buf")

            def do_tile(i, t0, n):
                hb = hbuf[i % 2]
                hkey = ("h", i % 2)
                rmsnorm(hb, hkey, G_MLP + layer, "hm", n, sq_t, rstd_t, out_of=lambda k: hm[:, k, 0:n])
                hmk = [("hm", k) for k in range(KD)]

                def ev_up(m, b, ps):
                    rt = r_t[m % 2]
                    S.add("act", I("activation", rt[:, 0:n], ps, AF.Relu), reads=(("ps", b),), writes=(("r", m % 2),))
                    S.add("dve", I("tensor_tensor", ff[:, m, 0:n], rt[:, 0:n], rt[:, 0:n], ALU.mult),
                          reads=(("r", m % 2),), writes=(("ff", m),))
                gemm(wu, "wu", KD, lambda k: hm[:, k, 0:n], hmk, range(DFF // 128), n, ev_up)
                ffk = [("ff", m) for m in range(DFF // 128)]

                def ev_dn(m, b, ps):
                    S.add("dve", I("tensor_tensor", hb[:, m, 0:n], hb[:, m, 0:n], ps, ALU.add),
                          reads=(("ps", b), hkey), writes=(hkey,))
                gemm(wd, "wd", DFF // 128, lambda k: ff[:, k, 0:n], ffk, range(KD), n, ev_dn)
                S.add("sp", I("dma_start", out=buf[:, t0:t0 + n].rearrange("(k p) n -> p k n", p=128),
                              in_=hb[:, :, 0:n]), reads=(hkey,), writes=(("buf", t0),), chan=f"st{i % 2}")

            load(0, tl[0][0], tl[0][1], tl[0][0])
            for i, (t0, n) in enumerate(tl):
                if i + 1 < len(tl):
                    load(i + 1, tl[i + 1][0], tl[i + 1][1], tl[i + 1][0])
                do_tile(i, t0, n)
            S.barrier()

        def ple_phase(layer, buf, start, total, p_src, do_kv=False, final=False):
            A.reset(PERSIST)
            ps_pool[0] = list(range(8))
            wgt = A.alloc([KD, D], BF16)
            wpj = A.alloc([2, D], BF16)
            load_w(wgt, w_gate[layer].rearrange("(k p) o -> p k o", p=128), "wgt")
            load_w(wpj, w_proj[layer].rearrange("(k p) o -> p k o", p=128), "wpj")
            NM = 512
            if do_kv:
                wk2 = A.alloc([KD, NKV, 128], BF16)
                wks2 = A.alloc([KD, NKV, 128], BF16)
                wv = A.alloc([KD, 256], BF16)
                for dup in range(2):
                    for k in range(KD):
                        load_w(wk2[:, k, :, 64 * dup:64 * dup + 64],
                               w_k[k * 128:(k + 1) * 128, :].rearrange("p (v d) -> p v d", d=64), "wk2")
                        load_w(wks2[:, k, :, 64 * dup:64 * dup + 64],
                               w_ks[k * 128:(k + 1) * 128, :].rearrange("p (v d) -> p v d", d=64), "wks2")
                load_w(wv, w_v.rearrange("(k p) o -> p k o", p=128), "wv")
                cs_t = A.alloc([2, NM], F32)
                kt_t = A.alloc([NKV, NM], BF16)
                vt_t = A.alloc([NM // 128, 256], BF16)
            hbuf = [A.alloc([KD, NM], F32) for _ in range(2)]
            pbuf = [A.alloc([2, NM], F32) for _ in range(2)]
            p_bf = A.alloc([2, NM], BF16)
            sq_t = A.alloc([KD, NM], BF16)
            rstd_t = A.alloc([NM], F32)
            hp = A.alloc([KD, NM], BF16)
            gate_t = A.alloc([NM], F32)
            tmp_t = A.alloc([NM], F32)
            tl = tiles_of(start, total % NM, NM) + tiles_of(start + total % NM, total - total % NM, NM)
            loadh = tile_loader(buf, hbuf, "h", "h", "buf")
            loadp = tile_loader(p_src, pbuf, "p", "p", "psrc")

            def load(i):
                t0, n = tl[i]
                loadh(i, t0, n, t0)
                loadp(i, t0, n, t0 - start)

            def do_tile(i, t0, n):
                hb = hbuf[i % 2]
                hkey = ("h", i % 2)
                pb = pbuf[i % 2]
                S.add("act", I("activation", p_bf[:, :, 0:n], pb[:, :, 0:n], AF.Copy),
                      reads=(("p", i % 2),), writes=("pbf",))
                rmsnorm(hb, hkey, G_PLE + layer, "hp", n, sq_t, rstd_t, out_of=lambda k: hp[:, k, 0:n])
                hpk = [("hp", k) for k in range(KD)]
                for m in range(KD):
                    bg = next_ps(); bp = next_ps()
                    psg = psb[bg][:, 0:n]; psp = psb[bp][:, 0:n]
                    S.add("pe", [I("matmul", psg, wgt[:, k, m * 128:(m + 1) * 128], hp[:, k, 0:n],
                                   start=(k == 0), stop=(k == KD - 1)) for k in range(KD)] +
                                [I("matmul", psp, wpj[:, 0, m * 128:(m + 1) * 128], p_bf[:, 0, 0:n], start=True, stop=False),
                                 I("matmul", psp, wpj[:, 1, m * 128:(m + 1) * 128], p_bf[:, 1, 0:n], start=False, stop=True)],
                          reads=["wgt", "wpj", "pbf"] + hpk, writes=(("ps", bg), ("ps", bp)))
                    S.add("act", I("activation", gate_t[:, 0:n], psg, AF.Sigmoid), reads=(("ps", bg),), writes=("gate",))
                    S.add("dve", I("tensor_tensor", tmp_t[:, 0:n], psp, gate_t[:, 0:n], ALU.mult),
                          reads=(("ps", bp), "gate"), writes=("tmp",))
                    S.add("dve", I("tensor_tensor", hb[:, m, 0:n], hb[:, m, 0:n], tmp_t[:, 0:n], ALU.add),
                          reads=("tmp", hkey), writes=(hkey,))
                if final:
                    rmsnorm(hb, hkey, G_FIN, "fin", n, sq_t, rstd_t, out_of=lambda k: hb[:, k, 0:n])
                    S.add("sp", I("dma_start", out=outT[:, t0 - start:t0 - start + n].rearrange("(k p) n -> p k n", p=128),
                                  in_=hb[:, :, 0:n]), reads=[hkey] + [("fin", k) for k in range(KD)],
                          writes=(("out", t0),), chan=f"st{i % 2}")
                else:
                    S.add("sp", I("dma_start", out=buf[:, t0:t0 + n].rearrange("(k p) n -> p k n", p=128),
                                  in_=hb[:, :, 0:n]), reads=(hkey,), writes=(("buf", t0),), chan=f"st{i % 2}")
                if do_kv:
                    a0 = t0 - start
                    S.add("sp", I("dma_start", out=cs_t[:, 0, 0:n], in_=cosT[:, a0:a0 + n]), writes=("cs",), chan="cs")
                    S.add("sp", I("dma_start", out=cs_t[:, 1, 0:n], in_=sinT[:, a0:a0 + n]), writes=("cs",), chan="cs")
                    rmsnorm(hb, hkey, G_KV, "hk", n, sq_t, rstd_t, out_of=lambda k: hp[:, k, 0:n])
                    hkk = [("hk", k) for k in range(KD)]
                    for kv in range(NKV):
                        b1 = next_ps(); b2 = next_ps()
                        ps1 = psb[b1][:, 0:n]; ps2 = psb[b2][:, 0:n]
                        S.add("pe", [I("matmul", ps1, wk2[:, k, kv, :], hp[:, k, 0:n], start=(k == 0), stop=(k == KD - 1))
                                     for k in range(KD)] +
                                    [I("matmul", ps2, wks2[:, k, kv, :], hp[:, k, 0:n], start=(k == 0), stop=(k == KD - 1))
                                     for k in range(KD)],
                              reads=["wk2", "wks2"] + hkk, writes=(("ps", b1), ("ps", b2)))
                        S.add("dve", I("tensor_tensor", gate_t[:, 0:n], ps1, cs_t[:, 0, 0:n], ALU.mult),
                              reads=(("ps", b1), "cs"), writes=("gate",))
                        S.add("dve", I("tensor_tensor", tmp_t[:, 0:n], ps2, cs_t[:, 1, 0:n], ALU.mult),
                              reads=(("ps", b2), "cs"), writes=("tmp",))
                        S.add("dve", I("tensor_tensor", kt_t[:, kv, 0:n], gate_t[:, 0:n], tmp_t[:, 0:n], ALU.add),
                              reads=("gate", "tmp"), writes=("ktt",))
                    S.add("sp", I("dma_start", out=kT2_d[:, :, a0:a0 + n], in_=kt_t[:, :, 0:n]),
                          reads=("ktt",), writes=("kT2d",), chan="kst")
                    for blk in range(n // 128):
                        b = next_ps()
                        ps = psb[b][:, 0:256]
                        S.add("pe", [I("matmul", ps, hp[:, k, blk * 128:(blk + 1) * 128], wv[:, k, :],
                                       start=(k == 0), stop=(k == KD - 1)) for k in range(KD)],
                              reads=["wv"] + hkk, writes=(("ps", b),))
                        S.add("act", I("activation", vt_t[:, blk, :], ps, AF.Copy), reads=(("ps", b),), writes=("vtt",))
                    S.add("sp", I("dma_start", out=v_d[:, a0 // 128:a0 // 128 + n // 128, :], in_=vt_t[:, 0:n // 128, :]),
                          reads=("vtt",), writes=("vd",), chan="vst")

            load(0)
            for i, (t0, n) in enumerate(tl):
                if i + 1 < len(tl):
                    load(i + 1)
                do_tile(i, t0, n)
            S.barrier()

        def attn_phase(layer):
            j = layer - 2
            A.reset(PERSIST)
            ps_pool[0] = list(range(6))
            wq = A.alloc([KD, D], BF16)
            wqs = A.alloc([KD, D], BF16)
            wo = A.alloc([KD, D], BF16)
            load_w(wq, w_q[j].rearrange("(k p) o -> p k o", p=128), "wq")
            load_w(wqs, w_qs[j].rearrange("(k p) o -> p k o", p=128), "wqs")
            load_w(wo, w_o[j].rearrange("(k p) o -> p k o", p=128), "wo")
            kT2 = A.alloc([NKV, OWNH], BF16)
            v_bf = A.alloc([NBLK, 256], BF16)
            S.add("sp", I("dma_start", out=kT2, in_=kT2_d), writes=("kT2",), chan="kld")
            S.add("sp", I("dma_start", out=v_bf, in_=v_d), writes=("vbf",), chan="kld")
            NM = 512
            hbuf = [A.alloc([KD, NM], F32) for _ in range(2)]
            cs_t = A.alloc([2, NM], F32)
            sq_t = A.alloc([KD, NM], BF16)
            rstd_t = A.alloc([NM], F32)
            hn = A.alloc([KD, NM], BF16)
            qT = A.alloc([KD, NM], BF16)
            t1 = A.alloc([NM], F32); t2 = A.alloc([NM], F32)
            sm_l = [A.alloc([2, 256], F32) for _ in range(2)]
            e_l = [A.alloc([2, 256], BF16) for _ in range(2)]
            eT_l = [A.alloc([2, 2, 128], BF16) for _ in range(2)]
            stat_l = [A.alloc([8], F32) for _ in range(2)]
            rinv = A.alloc([NH], F32)
            o_bf = A.alloc([D], BF16)
            oT = A.alloc([KD, NM], BF16)
            tl = tiles_of(HALO, OWN, NM)
            load = tile_loader(hB, hbuf, "h", "h", "buf")

            def do_tile(i, t0, n):
                hb = hbuf[i % 2]
                hkey = ("h", i % 2)
                S.add("sp", I("dma_start", out=cs_t[:, 0, 0:n], in_=cosT[:, t0:t0 + n]), writes=("cs",), chan="cs")
                S.add("sp", I("dma_start", out=cs_t[:, 1, 0:n], in_=sinT[:, t0:t0 + n]), writes=("cs",), chan="cs")
                rmsnorm(hb, hkey, G_MIX + layer, "hn", n, sq_t, rstd_t, out_of=lambda k: hn[:, k, 0:n])
                hnk = [("hn", k) for k in range(KD)]
                for m in range(KD):
                    b1 = next_ps(); b2 = next_ps()
                    ps1 = psb[b1][:, 0:n]; ps2 = psb[b2][:, 0:n]
                    S.add("pe", [I("matmul", ps1, wq[:, k, m * 128:(m + 1) * 128], hn[:, k, 0:n],
                                   start=(k == 0), stop=(k == KD - 1)) for k in range(KD)] +
                                [I("matmul", ps2, wqs[:, k, m * 128:(m + 1) * 128], hn[:, k, 0:n],
                                   start=(k == 0), stop=(k == KD - 1)) for k in range(KD)],
                          reads=["wq", "wqs"] + hnk, writes=(("ps", b1), ("ps", b2)))
                    S.add("dve", I("tensor_tensor", t1[:, 0:n], ps1, cs_t[:, 0, 0:n], ALU.mult),
                          reads=(("ps", b1), "cs"), writes=("t1",))
                    S.add("dve", I("tensor_tensor", t2[:, 0:n], ps2, cs_t[:, 1, 0:n], ALU.mult),
                          reads=(("ps", b2), "cs"), writes=("t2",))
                    S.add("dve", I("tensor_tensor", qT[:, m, 0:n], t1[:, 0:n], t2[:, 0:n], ALU.add),
                          reads=("t1", "t2"), writes=(("q", m),))
                for blk in range(n // 128):
                    ab = (t0 - HALO) // 128 + blk
                    msk = mask0_t if ab == 0 else maskG_t
                    mkey = "mask0" if ab == 0 else "maskG"
                    bo = [6, 7]

                    def qk_softmax(hp_, blk=blk, ab=ab, msk=msk, mkey=mkey):
                        par = hp_ % 2
                        sm, e_bf, stat = sm_l[par], e_l[par], stat_l[par]
                        bsl = [next_ps(), next_ps()]
                        for a in range(2):
                            S.add("pe", I("matmul", psb[bsl[a]][:, 0:256],
                                          qT[64 * a:64 * a + 64, hp_, blk * 128:(blk + 1) * 128],
                                          kT2[64 * a:64 * a + 64, (2 * hp_ + a) // 4, ab * 128:ab * 128 + 256],
                                          start=True, stop=True, tile_position=(64 * a, 0)),
                                  reads=(("q", hp_), "kT2"), writes=(("ps", bsl[a]),))
                        for a in range(2):
                            hd = 2 * hp_ + a
                            S.add("dve", I("scalar_tensor_tensor", sm[:, a, :], psb[bsl[a]][:, 0:256], 0.125, msk,
                                           ALU.mult, ALU.add),
                                  reads=(("ps", bsl[a]), mkey), writes=(("sm", par, a),))
                            S.add("dve", I("reduce_max", stat[:, a:a + 1], sm[:, a, :], axis=AX.X),
                                  reads=(("sm", par, a),), writes=(("mx", par, a),))
                            S.add("dve", I("tensor_scalar", stat[:, 2 + a:3 + a], stat[:, a:a + 1],
                                           sink_t[:, j, hd:hd + 1], -1.0, ALU.max, ALU.mult),
                                  reads=(("mx", par, a), "sink"), writes=(("ngm", par, a),))
                            S.add("dve", I("memset", stat[:, 4 + a:5 + a], 0.0), writes=(("rs", par, a),))
                            S.add("act", I("activation", e_bf[:, a, :], sm[:, a, :], AF.Exp, bias=stat[:, 2 + a:3 + a],
                                           scale=1.0, accum_out=stat[:, 4 + a:5 + a]),
                                  reads=(("sm", par, a), ("ngm", par, a)), writes=(("e", par, a), ("rs", par, a)))
                            S.add("act", I("activation", stat[:, 6 + a:7 + a], stat[:, 2 + a:3 + a], AF.Exp,
                                           bias=sink_t[:, j, hd:hd + 1], scale=1.0),
                                  reads=(("ngm", par, a), "sink"), writes=(("es", par, a),))
                            S.add("dve", I("tensor_tensor", stat[:, 4 + a:5 + a], stat[:, 4 + a:5 + a],
                                           stat[:, 6 + a:7 + a], ALU.add),
                                  reads=(("rs", par, a), ("es", par, a)), writes=(("rs", par, a),))
                            S.add("dve", I("reciprocal", rinv[:, hd:hd + 1], stat[:, 4 + a:5 + a]),
                                  reads=(("rs", par, a),), writes=(("rinv", hd),))

                    def tr_pv(hp_, ab=ab):
                        par = hp_ % 2
                        e_bf, eT = e_l[par], eT_l[par]
                        bt = next_ps()
                        pst = psb[bt][:].bitcast(BF16)[:, 0:512].rearrange("p (a c q) -> p a c q", a=2, c=2)
                        S.add("pe", [I("transpose", pst[:, a, c, :], e_bf[:, a, c * 128:(c + 1) * 128], ident_bf)
                                     for a in range(2) for c in range(2)],
                              reads=(("e", par, 0), ("e", par, 1), "identbf"), writes=(("ps", bt),))
                        S.add("act", I("activation", eT, pst, AF.Copy), reads=(("ps", bt),), writes=(("eT", par),))
                        pso = psb[bo[hp_ // 4]][:, 0:512].rearrange("p (h d) -> p h d", d=64)
                        S.add("pe", [I("matmul", pso[:, (2 * hp_ + a) % 8, :], eT[:, a, c, :],
                                       v_bf[:, ab + c, ((2 * hp_ + a) // 4) * 64:((2 * hp_ + a) // 4 + 1) * 64],
                                       start=(c == 0), stop=(c == 1)) for a in range(2) for c in range(2)],
                              reads=(("eT", par), "vbf"), writes=(("pso", hp_),))

                    qk_softmax(0)
                    for hp_ in range(NH // 2):
                        if hp_ + 1 < NH // 2:
                            qk_softmax(hp_ + 1)
                        tr_pv(hp_)
                    for half in range(2):
                        pso = psb[bo[half]][:, 0:512].rearrange("p (h d) -> p h d", d=64)
                        ov = o_bf[:, half * 512:(half + 1) * 512].rearrange("p (h d) -> p h d", d=64)
                        S.add("dve", I("tensor_tensor", ov, pso,
                                       rinv[:, half * 8:half * 8 + 8].unsqueeze(2).to_broadcast([128, 8, 64]), ALU.mult),
                              reads=[("pso", hp_) for hp_ in range(4 * half, 4 * half + 4)] +
                                    [("rinv", hd) for hd in range(8 * half, 8 * half + 8)],
                              writes=(("ob", half),) + tuple(("pso", hp_) for hp_ in range(4 * half, 4 * half + 4)))
                    for kk in range(KD):
                        bt = next_ps()
                        pst = psb[bt][:].bitcast(BF16)[:, 0:128]
                        S.add("pe", I("transpose", pst, o_bf[:, kk * 128:(kk + 1) * 128], ident_bf),
                              reads=(("ob", kk // 4), "identbf"), writes=(("ps", bt),))
                        S.add("act", I("activation", oT[:, kk, blk * 128:(blk + 1) * 128], pst, AF.Copy),
                              reads=(("ps", bt),), writes=(("oT", kk),))
                oTk = [("oT", kk) for kk in range(KD)]

                def ev_o(m, b, ps):
                    S.add("dve", I("tensor_tensor", hb[:, m, 0:n], hb[:, m, 0:n], ps, ALU.add),
                          reads=(("ps", b), hkey), writes=(hkey,))
                gemm(wo, "wo", KD, lambda k: oT[:, k, 0:n], oTk, range(KD), n, ev_o)
                S.add("sp", I("dma_start", out=hB[:, t0:t0 + n].rearrange("(k p) n -> p k n", p=128),
                              in_=hb[:, :, 0:n]), reads=(hkey,), writes=(("buf", t0),), chan=f"st{i % 2}")

            load(0, tl[0][0], tl[0][1], tl[0][0])
            for i, (t0, n) in enumerate(tl):
                if i + 1 < len(tl):
                    load(i + 1, tl[i + 1][0], tl[i + 1][1], tl[i + 1][0])
                do_tile(i, t0, n)
            S.barrier()

        S.barrier()
        stages = STAGES
        if stages >= 1:
            s5_phase(0, xT, hA, 0)
        if stages >= 2:
            mlp_phase(0, hA, 0, WIN)
        if stages >= 3:
            ple_phase(0, hA, 0, WIN, p0T)
        if stages >= 4:
            s5_phase(1, hA, hB, PRE)
        if stages >= 5:
            mlp_phase(1, hB, 0, OWNH)
        if stages >= 6:
            ple_phase(1, hB, 0, OWNH, pLT[0], do_kv=True)
        if stages >= 7:
            attn_phase(2)
        if stages >= 8:
            mlp_phase(2, hB, HALO, OWN)
        if stages >= 9:
            ple_phase(2, hB, HALO, OWN, pLT[1][:, HALO:OWNH])
        if stages >= 10:
            attn_phase(3)
        if stages >= 11:
            mlp_phase(3, hB, HALO, OWN)
        if stages >= 12:
            ple_phase(3, hB, HALO, OWN, pLT[2][:, HALO:OWNH], final=True)
        if stages < 12:
            srcb = hA[:, WIN - OWN:WIN] if stages <= 3 else hB[:, HALO:OWNH]
            S.add("sp", I("dma_start", out=outT, in_=srcb), writes=("dbg",), chan="dbg")
        S.finish()
    _DBG_MAPS[(WIN, OWN)] = dbg_map
    return nc


STAGES = 12
ATT_LEVEL = 6
DBGW = 0
PREP_ONLY = False
_DBG_MAPS = {}


def _rope_tables(pos):
    inv = ROPE_THETA ** (-np.arange(0, ROT, 2, dtype=np.float32) / ROT)
    ang = pos.astype(np.float32)[None, :] * inv[:, None].astype(np.float32)
    cos = np.cos(ang).astype(np.float32)
    sin = np.sin(ang).astype(np.float32)
    n = pos.shape[0]
    C = np.ones((64, n), np.float32)
    Sg = np.zeros((64, n), np.float32)
    C[0:8] = cos; C[8:16] = cos
    Sg[0:8] = -sin; Sg[8:16] = sin
    return np.concatenate([C, C], 0), np.concatenate([Sg, Sg], 0)


def _swap_cols(w, nheads):
    w = w.reshape(w.shape[0], nheads, HD)
    ws = w.copy()
    ws[:, :, 0:8] = w[:, :, 8:16]
    ws[:, :, 8:16] = w[:, :, 0:8]
    return np.ascontiguousarray(ws.reshape(w.shape[0], nheads * HD))


def _masks():
    qi = np.arange(128)[:, None] + 128
    kj = np.arange(256)[None, :]
    band = (kj <= qi) & (qi - kj < 128)
    mg = np.where(band, 0.0, -30000.0).astype(np.float32)
    m0 = np.where(band & (kj >= 128), 0.0, -30000.0).astype(np.float32)
    return m0, mg


_PROG_CACHE = {}


def run_config(inputs, NQ, n_cores=8):
    x = np.asarray(inputs["x"], np.float32)
    p = np.asarray(inputs["p"], np.float32)
    Bsz, SEQ, _ = x.shape
    OWN = SEQ // NQ
    WIN = SEQ
    HALO = 128
    OWNH = OWN + HALO
    key = (WIN, OWN)
    if key not in _PROG_CACHE:
        _PROG_CACHE[key] = build_program(WIN, OWN, HALO)
    nc = _PROG_CACHE[key]
    m0, mg = _masks()
    shared = {}
    for name in ("norm_mix", "ssm_lambda_re", "ssm_lambda_im", "ssm_log_dt", "ssm_b_re", "ssm_b_im",
                 "ssm_c_re", "ssm_c_im", "ssm_d", "ssm_w_glu", "kv_norm", "w_k", "w_v", "w_q",
                 "attn_sinks", "w_o", "norm_mlp", "w_up", "w_down", "norm_ple", "w_ple_gate",
                 "w_ple_proj", "norm_final"):
        shared[name] = np.ascontiguousarray(np.asarray(inputs[name], np.float32))
    shared["w_ks"] = _swap_cols(shared["w_k"], NKV)
    shared["w_qs"] = np.stack([_swap_cols(shared["w_q"][i], NH) for i in range(2)])
    shared["ident"] = np.eye(128, dtype=np.float32)
    bd = np.zeros((128, 128), np.float32)
    for g in range(8):
        bd[16 * g:16 * g + 16, 16 * g:16 * g + 16] = 1.0
    shared["bdmask"] = bd
    shared["maskG"] = mg
    in_maps = []
    cores = []
    for c in range(n_cores):
        cc = c % (Bsz * NQ)
        b, q = cc // NQ, cc % NQ
        cores.append((b, q))
        own_end = (q + 1) * OWN
        w0 = own_end - WIN
        xw = np.zeros((WIN, D), np.float32)
        pw = np.zeros((WIN, PLE), np.float32)
        lo = max(w0, 0)
        xw[lo - w0:] = x[b, lo:own_end]
        pw[lo - w0:] = p[0, b, lo:own_end]
        h0 = q * OWN - HALO
        pl = np.zeros((3, OWNH, PLE), np.float32)
        lo2 = max(h0, 0)
        pl[:, lo2 - h0:] = p[1:4, b, lo2:own_end]
        pos = np.arange(h0, own_end)
        cosT, sinT = _rope_tables(pos)
        m = dict(shared)
        m["xT"] = np.ascontiguousarray(xw.T)
        m["p0T"] = np.ascontiguousarray(pw.T)
        m["pLT"] = np.ascontiguousarray(pl.transpose(0, 2, 1))
        m["cosT"] = cosT
        m["sinT"] = sinT
        m["mask0"] = m0 if q == 0 else mg
        in_maps.append(m)
    if PREP_ONLY:
        return nc, in_maps
    res = run_bass_kernel_spmd(nc, in_maps, core_ids=list(range(n_cores)))
    if DBGW:
        global LAST_DBG
        LAST_DBG = ([r["dbg"] for r in res.results], _DBG_MAPS[key])
    out = np.zeros((Bsz, SEQ, D), np.float32)
    for c in range(min(n_cores, Bsz * NQ)):
        b, q = cores[c]
        out[b, q * OWN:(q + 1) * OWN] = res.results[c]["outT"].T
    return out


def kernel(**inputs):
    return run_config(inputs, NQ=4, n_cores=8)
```

```python
import math
from contextlib import ExitStack

import numpy as np
import concourse.bass as bass
import concourse.mybir as mybir
from concourse.bass_utils import run_bass_kernel_spmd

F32 = mybir.dt.float32
BF16 = mybir.dt.bfloat16
AF = mybir.ActivationFunctionType
ALU = mybir.AluOpType
AX = mybir.AxisListType

D = 1024
KD = 8
DFF = 4096
PLE = 256
NH = 16
NKV = 4
HD = 64
T = 8
NPAIR = 32
EPS = 1e-6
ROPE_THETA = 500000.0
ROT = 16

SAME_ENGINE_SYNC = True


class Sched:
    ENG = ("pe", "act", "dve", "pool", "sp")

    def __init__(self, nc, stack):
        self.nc = nc
        self.stack = stack
        self.prog = {e: [] for e in self.ENG}
        self.sem = {}
        self.count = {}
        self.unit = {}
        self.last_write = {}
        self.readers = {}
        self.waited = {e: {} for e in self.ENG}
        for e in self.ENG:
            self._stream(e, 1)

    def _stream(self, name, unit):
        if name not in self.sem:
            self.sem[name] = self.stack.enter_context(self.nc.semaphore("s_" + name.replace(":", "_")))
            self.count[name] = 0
            self.unit[name] = unit
        return name

    def _wait(self, engine, stream, value):
        if self.waited[engine].get(stream, 0) >= value:
            return
        self.waited[engine][stream] = value
        sem = self.sem[stream]
        self.prog[engine].append(lambda eng, sem=sem, value=value: eng.wait_ge(sem, value))

    def add(self, engine, emit, reads=(), writes=(), chan=None):
        stream = engine if chan is None else self._stream("dma:" + chan, 16)
        deps = {}
        for k in reads:
            lw = self.last_write.get(k)
            if lw is not None:
                deps[lw[0]] = max(deps.get(lw[0], 0), lw[1] + 1)
        for k in writes:
            lw = self.last_write.get(k)
            if lw is not None:
                deps[lw[0]] = max(deps.get(lw[0], 0), lw[1] + 1)
            for r in self.readers.get(k, ()):
                deps[r[0]] = max(deps.get(r[0], 0), r[1] + 1)
        if chan is not None and self.count[stream] > 0:
            deps[stream] = max(deps.get(stream, 0), self.count[stream])
        for s, n in deps.items():
            if s == engine and chan is None:
                if engine == "pe" or not SAME_ENGINE_SYNC:
                    continue
            self._wait(engine, s, n * self.unit[s])
        sem = self.sem[stream]
        unit = self.unit[stream]
        ops = [emit] if isinstance(emit, tuple) else list(emit)
        assert ops and all(isinstance(o, tuple) for o in ops)

        def run(eng, ops=ops, sem=sem, unit=unit):
            last = None
            for (meth, args, kw) in ops:
                last = getattr(eng, meth)(*args, **kw)
            last.then_inc(sem, unit)
        self.prog[engine].append(run)
        idx = self.count[stream]
        self.count[stream] = idx + 1
        for k in writes:
            self.last_write[k] = (stream, idx)
            self.readers[k] = []
        for k in reads:
            self.readers.setdefault(k, []).append((stream, idx))

    def barrier(self):
        for e in self.ENG:
            for s in self.sem:
                if s != e and self.count[s] > 0:
                    self._wait(e, s, self.count[s] * self.unit[s])
        self.last_write.clear()
        self.readers.clear()

    def finish(self):
        self.barrier()
        nc = self.nc
        with nc.Block() as block:
            @block.tensor
            def _(eng):
                for f in self.prog["pe"]:
                    f(eng)

            @block.scalar
            def _(eng):
                for f in self.prog["act"]:
                    f(eng)

            @block.vector
            def _(eng):
                for f in self.prog["dve"]:
                    f(eng)

            @block.gpsimd
            def _(eng):
                for f in self.prog["pool"]:
                    f(eng)

            @block.sync
            def _(eng):
                for f in self.prog["sp"]:
                    f(eng)


def I(meth, *args, **kw):
    return (meth, args, kw)


class Arena:
    def __init__(self, ap, words):
        self.ap = ap
        self.words = words
        self.off = 0

    def reset(self, to=0):
        self.off = to

    def alloc(self, free_shape, dtype):
        n = 1
        for s in free_shape:
            n *= s
        w = n if dtype == F32 else (n + 1) // 2
        w = (w + 7) // 8 * 8
        assert self.off + w <= self.words, f"SBUF arena overflow {self.off}+{w}>{self.words}"
        v = self.ap[:, self.off:self.off + w]
        self.off += w
        if dtype != F32:
            v = v.bitcast(dtype)
        v = v[:, 0:n]
        if len(free_shape) == 1:
            return v
        names = " ".join(f"a{i}" for i in range(len(free_shape)))
        kw = {f"a{i}": s for i, s in enumerate(free_shape[:-1])}
        return v.rearrange(f"p ({names}) -> p {names}", **kw)


def tiles_of(start, total, size):
    out = []
    t = start
    end = start + total
    while t < end:
        n = min(size, end - t)
        out.append((t, n))
        t += n
    return out


def build_program(WIN, OWN, HALO=128, ARENA_WORDS=48 * 1024):
    OWNH = OWN + HALO
    PRE = WIN - OWNH
    assert PRE >= 0 and PRE % 128 == 0 and OWN % 512 == 0
    NBLK = OWNH // 128
    nc = bass.Bass("TRN2", target_bir_lowering=False)
    dt_in = lambda name, shape: nc.dram_tensor(name, list(shape), F32, kind="ExternalInput").ap()
    xT = dt_in("xT", [D, WIN])
    p0T = dt_in("p0T", [PLE, WIN])
    pLT = dt_in("pLT", [3, PLE, OWNH])
    cosT = dt_in("cosT", [128, OWNH])
    sinT = dt_in("sinT", [128, OWNH])
    mask0 = dt_in("mask0", [128, 256])
    maskG = dt_in("maskG", [128, 256])
    ident_d = dt_in("ident", [128, 128])
    bdmask_d = dt_in("bdmask", [128, 128])
    norm_mix = dt_in("norm_mix", [4, D])
    lam_re = dt_in("ssm_lambda_re", [2, 64, 64])
    lam_im = dt_in("ssm_lambda_im", [2, 64, 64])
    log_dt = dt_in("ssm_log_dt", [2, 64])
    b_re = dt_in("ssm_b_re", [2, 64, 64, 16])
    b_im = dt_in("ssm_b_im", [2, 64, 64, 16])
    c_re = dt_in("ssm_c_re", [2, 64, 16, 64])
    c_im = dt_in("ssm_c_im", [2, 64, 16, 64])
    ssm_d = dt_in("ssm_d", [2, D])
    w_glu = dt_in("ssm_w_glu", [2, D, 2 * D])
    kv_norm = dt_in("kv_norm", [D])
    w_k = dt_in("w_k", [D, 256])
    w_ks = dt_in("w_ks", [D, 256])
    w_v = dt_in("w_v", [D, 256])
    w_q = dt_in("w_q", [2, D, D])
    w_qs = dt_in("w_qs", [2, D, D])
    sinks = dt_in("attn_sinks", [2, NH])
    w_o = dt_in("w_o", [2, D, D])
    norm_mlp = dt_in("norm_mlp", [4, D])
    w_up = dt_in("w_up", [4, D, DFF])
    w_down = dt_in("w_down", [4, DFF, D])
    norm_ple = dt_in("norm_ple", [4, D])
    w_gate = dt_in("w_ple_gate", [4, D, D])
    w_proj = dt_in("w_ple_proj", [4, PLE, D])
    norm_final = dt_in("norm_final", [D])
    outT = nc.dram_tensor("outT", [D, OWN], F32, kind="ExternalOutput").ap()
    hA = nc.dram_tensor("hA", [D, WIN], F32, kind="Internal").ap()
    hB = nc.dram_tensor("hB", [D, OWNH], F32, kind="Internal").ap()
    kT2_d = nc.dram_tensor("kT2_d", [128, NKV, OWNH], BF16, kind="Internal").ap()
    dbg_d = nc.dram_tensor("dbg", [128, DBGW], F32, kind="ExternalOutput").ap() if DBGW else None
    dbg_off = [0]
    dbg_map = {}
    v_d = nc.dram_tensor("v_d", [128, NBLK, 256], BF16, kind="Internal").ap()

    stack = ExitStack()
    with stack:
        S = Sched(nc, stack)
        arena_t = stack.enter_context(nc.sbuf_tensor("arena", [128, ARENA_WORDS], F32))
        A = Arena(arena_t[:], ARENA_WORDS)
        psb = [stack.enter_context(nc.psum_tensor(f"ps{i}", [128, 512], F32)) for i in range(8)]
        ps_pool = [list(range(8))]
        ps_rr = [0]

        def next_ps():
            pool = ps_pool[0]
            i = pool[ps_rr[0] % len(pool)]
            ps_rr[0] += 1
            return i

        ident = A.alloc([128], F32)
        ident_bf = A.alloc([128], BF16)
        bdmask = A.alloc([128], F32)
        ones_bf = A.alloc([128], BF16)
        gam = A.alloc([14, KD], F32)
        dvec = A.alloc([2, KD], F32)
        sink_t = A.alloc([2, NH], F32)
        mask0_t = A.alloc([256], F32)
        maskG_t = A.alloc([256], F32)
        xcar = A.alloc([2, NPAIR], F32)
        eps_t = A.alloc([1], F32)
        hpi_t = A.alloc([1], F32)
        PERSIST = A.off

        def dump(name, ap, keys):
            if dbg_d is None or name in dbg_map:
                return
            w = 1
            for d_ in ap.shape[1:]:
                w *= d_
            if dbg_off[0] + w > DBGW:
                return
            dst = dbg_d[:, dbg_off[0]:dbg_off[0] + w]
            if len(ap.shape) > 2:
                names = " ".join(f"a{i}" for i in range(len(ap.shape) - 1))
                kw = {f"a{i}": d_ for i, d_ in enumerate(ap.shape[1:-1])}
                dst = dst.rearrange(f"p ({names}) -> p {names}", **kw)
            dbg_map[name] = (dbg_off[0], tuple(ap.shape[1:]))
            dbg_off[0] += w
            S.add("sp" if ap.dtype == F32 else "pool", I("dma_start", out=dst, in_=ap), reads=tuple(keys),
                  writes=(("dbg", name),), chan="dbgc" if ap.dtype == F32 else "dbgp")

        def dma_in(eng, out, in_, key, chan, **kw):
            S.add(eng, I("dma_start", out=out, in_=in_, **kw), reads=(), writes=(key,), chan=chan)

        dma_in("sp", ident, ident_d, "ident", "c0")
        dma_in("sp", bdmask, bdmask_d, "bdmask", "c0")
        dma_in("sp", mask0_t, mask0, "mask0", "c0")
        dma_in("sp", maskG_t, maskG, "maskG", "c0")
        gsrc = [norm_mix[i] for i in range(4)] + [norm_mlp[i] for i in range(4)] + \
               [norm_ple[i] for i in range(4)] + [kv_norm, norm_final]
        for i, g in enumerate(gsrc):
            dma_in("sp", gam[:, i, :], g.rearrange("(k p) -> p k", p=128), "gam", "c0",
                   allow_slow_non_contiguous=True)
        for i in range(2):
            dma_in("sp", dvec[:, i, :], ssm_d[i].rearrange("(k p) -> p k", p=128), "dvec", "c0",
                   allow_slow_non_contiguous=True)
            dma_in("sp", sink_t[:, i, :], sinks[i].partition_broadcast(128), "sink", "c0")
        S.add("dve", I("memset", ones_bf, 1.0 / D), writes=("ones",))
        S.add("dve", I("tensor_copy", ident_bf, ident), reads=("ident",), writes=("identbf",))
        S.add("dve", I("memset", xcar, 0.0), writes=("xcar",))
        S.add("dve", I("memset", eps_t, EPS), writes=("eps",))
        S.add("dve", I("memset", hpi_t, math.pi / 2), writes=("hpi",))
        G_MIX, G_MLP, G_PLE, G_KV, G_FIN = 0, 4, 8, 12, 13

        def load_w(dst, src_ap, key, chan="w"):
            S.add("pool", I("dma_start", out=dst, in_=src_ap), writes=(key,), chan=chan)

        def rmsnorm(h_t, hkey, gidx, okey, n, sq_t, rstd_t, out_of, view=None):
            vw = (lambda a: a) if view is None else view
            for k in range(KD):
                S.add("act", I("activation", sq_t[:, k, 0:n], h_t[:, k, 0:n], AF.Square),
                      reads=(hkey,), writes=(("sq", k),))
            b = next_ps()
            ps = psb[b][:, 0:n]
            S.add("pe", [I("matmul", ps, ones_bf, sq_t[:, k, 0:n], start=(k == 0), stop=(k == KD - 1))
                         for k in range(KD)],
                  reads=[("sq", k) for k in range(KD)] + ["ones"], writes=(("ps", b),))
            S.add("act", I("activation", rstd_t[:, 0:n], ps, AF.Sqrt, bias=eps_t, scale=1.0),
                  reads=(("ps", b), "eps"), writes=("rstd",))
            S.add("dve", I("reciprocal", rstd_t[:, 0:n], rstd_t[:, 0:n]), reads=("rstd",), writes=("rstd",))
            for k in range(KD):
                S.add("dve", I("scalar_tensor_tensor", out_of(k), vw(h_t[:, k, 0:n]), gam[:, gidx, k:k + 1],
                               vw(rstd_t[:, 0:n]), ALU.mult, ALU.mult),
                      reads=(hkey, "rstd", "gam"), writes=((okey, k),))

        def gemm(w_t, wkey, kin, act_of, act_keys, mlist, n, evac):
            for m in mlist:
                b = next_ps()
                ps = psb[b][:, 0:n]
                S.add("pe", [I("matmul", ps, w_t[:, k, m * 128:(m + 1) * 128], act_of(k),
                               start=(k == 0), stop=(k == kin - 1)) for k in range(kin)],
                      reads=[wkey] + list(act_keys), writes=(("ps", b),))
                evac(m, b, ps)

        def tile_loader(src_ap, bufs, keyname, chan, src_key, nslots=2):
            def load(i, t0, n, col0):
                hb = bufs[i % nslots]
                S.add("sp", I("dma_start", out=hb[:, :, 0:n],
                              in_=src_ap[:, col0:col0 + n].rearrange("(k p) n -> p k n", p=128)),
                      reads=((src_key, t0),), writes=((keyname, i % nslots),), chan=f"{chan}{i % nslots}")
            return load

        def s5_prep(li, SIN, OUT, KTt, alpha):
            base = A.off
            lr = A.alloc([NPAIR], F32); lim = A.alloc([NPAIR], F32); dtt = A.alloc([NPAIR], F32)
            t1 = A.alloc([NPAIR], F32); t2 = A.alloc([NPAIR], F32); t3 = A.alloc([NPAIR], F32)
            akr = A.alloc([T + 1, NPAIR], F32); aki = A.alloc([T + 1, NPAIR], F32)
            cfr = A.alloc([NPAIR], F32); cfi = A.alloc([NPAIR], F32)
            bR = A.alloc([NPAIR, 16], F32); bI = A.alloc([NPAIR, 16], F32)
            bbr = A.alloc([NPAIR, 16], F32); bbi = A.alloc([NPAIR, 16], F32)
            zr = A.alloc([NPAIR, 16], F32); zi = A.alloc([NPAIR, 16], F32)
            tb1 = A.alloc([NPAIR, 16], F32); tb2 = A.alloc([NPAIR, 16], F32)
            cnat = A.alloc([2, KD, 128], F32)
            cR = A.alloc([NPAIR, 16], F32); cI = A.alloc([NPAIR, 16], F32)
            zpr = A.alloc([128], F32); zpi = A.alloc([128], F32)
            cpr = A.alloc([KD, 128], F32); cpi = A.alloc([KD, 128], F32)
            PK = "prep"

            def l1(src):
                return src.rearrange("(q e) n -> e n q", e=2)
            for e_ in range(2):
                sl = slice(64 * e_, 64 * e_ + 64)
                dma_in("sp", lr[sl, :], l1(lam_re[li])[e_], PK, "c0", allow_slow_non_contiguous=True)
                dma_in("sp", lim[sl, :], l1(lam_im[li])[e_], PK, "c0", allow_slow_non_contiguous=True)
                dma_in("sp", dtt[sl, :], log_dt[li].rearrange("(q e) -> e q", e=2)[e_].partition_broadcast(64),
                       PK, "c0", allow_slow_non_contiguous=True)
                dma_in("sp", bR[sl, :, :], b_re[li].rearrange("(q e) n h -> e n q h", e=2)[e_], PK, "c0",
                       allow_slow_non_contiguous=True)
                dma_in("sp", bI[sl, :, :], b_im[li].rearrange("(q e) n h -> e n q h", e=2)[e_], PK, "c0",
                       allow_slow_non_contiguous=True)
            for r, csrc in enumerate((c_re, c_im)):
                for dup in range(2):
                    dma_in("sp", cnat[:, r, :, 64 * dup:64 * dup + 64],
                           csrc[li].rearrange("(k g) h n -> (g h) k n", k=KD), PK, "c0")

            def V(eng, inst, r=(PK,), w=(PK,)):
                S.add(eng, inst, reads=r, writes=w)

            def TT(o, a, b, op):
                V("dve", I("tensor_tensor", o, a, b, op))
            def exp_small(o, z):
                V("dve", I("tensor_scalar", o, z, 1.0 / 6, 1.0, ALU.mult, ALU.add))
                for dv in (5.0, 4.0, 3.0, 2.0, 1.0):
                    TT(o, o, z, ALU.mult)
                    V("dve", I("tensor_scalar", o, o, 1.0 / dv, 1.0, ALU.mult, ALU.add))
            V("dve", I("tensor_scalar_mul", t3, dtt, 1.0 / 64))
            exp_small(dtt, t3)
            for _ in range(6):
                TT(dtt, dtt, dtt, ALU.mult)
            TT(t3, lr, dtt, ALU.mult)
            exp_small(t1, t3)
            TT(t2, lim, dtt, ALU.mult)
            V("act", I("activation", aki[:, 1, :], t2, AF.Sin, scale=1.0 / 16), r=(PK, "hpi"))
            V("act", I("activation", akr[:, 1, :], t2, AF.Sin, bias=hpi_t, scale=1.0 / 16), r=(PK, "hpi"))
            for _ in range(4):
                TT(t3, akr[:, 1, :], aki[:, 1, :], ALU.mult)
                TT(akr[:, 1, :], akr[:, 1, :], akr[:, 1, :], ALU.mult)
                TT(aki[:, 1, :], aki[:, 1, :], aki[:, 1, :], ALU.mult)
                TT(akr[:, 1, :], akr[:, 1, :], aki[:, 1, :], ALU.subtract)
                V("dve", I("tensor_scalar_mul", aki[:, 1, :], t3, 2.0))
            TT(t3, akr[:, 1, :], akr[:, 1, :], ALU.mult)
            TT(t2, aki[:, 1, :], aki[:, 1, :], ALU.mult)
            TT(t3, t3, t2, ALU.add)
            V("dve", I("tensor_scalar", t3, t3, -0.5, 1.5, ALU.mult, ALU.add))
            TT(t1, t1, t3, ALU.mult)
            TT(akr[:, 1, :], akr[:, 1, :], t1, ALU.mult)
            TT(aki[:, 1, :], aki[:, 1, :], t1, ALU.mult)
            V("dve", I("memset", akr[:, 0, :], 1.0))
            V("dve", I("memset", aki[:, 0, :], 0.0))
            for k in range(2, T + 1):
                TT(t1, akr[:, k - 1, :], akr[:, 1, :], ALU.mult)
                TT(t2, aki[:, k - 1, :], aki[:, 1, :], ALU.mult)
                TT(akr[:, k, :], t1, t2, ALU.subtract)
                TT(t1, akr[:, k - 1, :], aki[:, 1, :], ALU.mult)
                TT(t2, aki[:, k - 1, :], akr[:, 1, :], ALU.mult)
                TT(aki[:, k, :], t1, t2, ALU.add)
            dump("dtt", dtt, [PK]); dump("akr", akr, [PK]); dump("aki", aki, [PK])
            V("dve", I("tensor_copy", alpha[:, 0, :], akr[:, T, :]), w=(PK, "alpha"))
            V("dve", I("tensor_copy", alpha[:, 1, :], aki[:, T, :]), w=(PK, "alpha"))
            V("dve", I("tensor_scalar_add", t1, akr[:, 1, :], -1.0))
            TT(t2, lr, lr, ALU.mult)
            TT(t3, lim, lim, ALU.mult)
            TT(t2, t2, t3, ALU.add)
            V("dve", I("reciprocal", t2, t2))
            TT(cfr, t1, lr, ALU.mult)
            TT(t3, aki[:, 1, :], lim, ALU.mult)
            TT(cfr, cfr, t3, ALU.add)
            TT(cfr, cfr, t2, ALU.mult)
            TT(cfi, aki[:, 1, :], lr, ALU.mult)
            TT(t3, t1, lim, ALU.mult)
            TT(cfi, cfi, t3, ALU.subtract)
            TT(cfi, cfi, t2, ALU.mult)

            def bc(v):
                return v.unsqueeze(2).to_broadcast([128, NPAIR, 16])

            def cmul(outr, outi, sr, si, vr, vi):
                TT(tb1, vr, bc(sr), ALU.mult)
                TT(tb2, vi, bc(si), ALU.mult)
                TT(outr, tb1, tb2, ALU.subtract)
                TT(tb1, vi, bc(sr), ALU.mult)
                TT(tb2, vr, bc(si), ALU.mult)
                TT(outi, tb1, tb2, ALU.add)
            cmul(bbr, bbi, cfr, cfi, bR, bI)
            dump("cfr", cfr, [PK]); dump("cfi", cfi, [PK]); dump("bbr", bbr, [PK]); dump("bbi", bbi, [PK])

            for r, dstc in enumerate((cR, cI)):
                for k in range(KD):
                    b = next_ps()
                    ps = psb[b][:, 0:128]
                    S.add("pe", I("transpose", ps, cnat[:, r, k, :], ident),
                          reads=(PK, "ident"), writes=(("ps", b),))
                    psv = ps.rearrange("p (q e h) -> p q e h", q=4, e=2)
                    S.add("act", I("activation", dstc[0:64, 4 * k:4 * k + 4, :], psv[0:64, :, 0, :], AF.Copy),
                          reads=(("ps", b),), writes=(PK,))
                    S.add("act", I("activation", dstc[64:128, 4 * k:4 * k + 4, :], psv[64:128, :, 1, :], AF.Copy),
                          reads=(("ps", b),), writes=(PK,))
            dump("cR", cR, [PK]); dump("cI", cI, [PK])
            S.add("pool", I("memset", OUT, 0.0), writes=("OUT",))
            for j in range(T):
                cmul(zr, zi, akr[:, j + 1, :], aki[:, j + 1, :], cR, cI)
                for e_ in range(2):
                    sl = slice(64 * e_, 64 * e_ + 64)
                    S.add("act", I("activation", OUT[sl, :, j, 0, 16 * e_:16 * e_ + 16], zr[sl, :, :], AF.Copy),
                          reads=(PK,), writes=("OUT",))
                    S.add("act", I("activation", OUT[sl, :, j, 1, 16 * e_:16 * e_ + 16], zi[sl, :, :], AF.Copy,
                                   scale=-1.0), reads=(PK,), writes=("OUT",))
            V("dve", I("memset", cpr, 0.0))
            V("dve", I("memset", cpi, 0.0))
            for k in range(KD):
                for e_ in range(2):
                    sl = slice(64 * e_, 64 * e_ + 64)
                    dv_r = cpr[:, k, :].rearrange("p (q e h) -> p q e h", q=4, e=2)
                    dv_i = cpi[:, k, :].rearrange("p (q e h) -> p q e h", q=4, e=2)
                    V("dve", I("tensor_copy", dv_r[sl, :, e_, :], cR[sl, 4 * k:4 * k + 4, :]))
                    V("dve", I("tensor_scalar_mul", dv_i[sl, :, e_, :], cI[sl, 4 * k:4 * k + 4, :], -1.0))
            V("dve", I("memset", zpr, 0.0))
            V("dve", I("memset", zpi, 0.0))
            zvr = zpr.rearrange("p (q e h) -> p q e h", q=4, e=2)
            zvi = zpi.rearrange("p (q e h) -> p q e h", q=4, e=2)
            for kk in range(T):
                cmul(zr, zi, akr[:, kk, :], aki[:, kk, :], bbr, bbi)
                s_ = T - 1 - kk
                for k in range(KD):
                    for e_ in range(2):
                        sl = slice(64 * e_, 64 * e_ + 64)
                        V("dve", I("tensor_copy", zvr[sl, :, e_, :], zr[sl, 4 * k:4 * k + 4, :]))
                        V("dve", I("tensor_copy", zvi[sl, :, e_, :], zi[sl, 4 * k:4 * k + 4, :]))
                    for part, zp in enumerate((zpr, zpi)):
                        b = next_ps()
                        ps = psb[b][:, 0:128]
                        S.add("pe", I("transpose", ps, zp, ident), reads=(PK, "ident"), writes=(("ps", b),))
                        S.add("act", I("activation", SIN[:, k, s_, part, :], ps, AF.Copy),
                              reads=(("ps", b),), writes=("SIN",))
                    b = next_ps()
                    ps = psb[b][:, 0:128]
                    S.add("pe", [I("matmul", ps, zpr, cpr[:, k, :], start=True, stop=False),
                                 I("matmul", ps, zpi, cpi[:, k, :], start=False, stop=True)],
                          reads=(PK,), writes=(("ps", b),))
                    S.add("dve", I("tensor_tensor", KTt[:, k, kk, :], ps, bdmask, ALU.mult),
                          reads=(("ps", b), "bdmask"), writes=("KT", PK))
            A.reset(base)

        def s5_phase(li, src, dst, out_start):
            A.reset(PERSIST)
            ps_pool[0] = list(range(8))
            SIN = A.alloc([KD, T, 2, 128], BF16)
            OUT = A.alloc([NPAIR, T, 2, 32], BF16)
            KTt = A.alloc([KD, T, 128], BF16)
            alpha = A.alloc([2, NPAIR], F32)
            wg = A.alloc([KD, 2 * D], BF16)
            load_w(wg, w_glu[li].rearrange("(k p) o -> p k o", p=128), "wglu")
            s5_prep(li, SIN, OUT, KTt, alpha)
            S.barrier()
            NMAX = 256
            CM = NMAX // T
            hbuf = [A.alloc([KD, NMAX], F32) for _ in range(3)]
            sq_t = A.alloc([KD, NMAX], BF16)
            rstd_t = A.alloc([NMAX], F32)
            u_bufs = [A.alloc([KD, NMAX], BF16) for _ in range(2)]
            S_bufs = [A.alloc([2, NPAIR, CM], F32) for _ in range(2)]
            X = A.alloc([2, NPAIR, CM + 1], F32)
            X_bf = A.alloc([2, NPAIR, CM], BF16)
            tA = A.alloc([NPAIR], F32); tB = A.alloc([NPAIR], F32)
            tC = A.alloc([NPAIR], F32); tD = A.alloc([NPAIR], F32)
            v_t = A.alloc([NMAX], F32); w_t2 = A.alloc([NMAX], F32); sg_t = A.alloc([NMAX], F32)
            g_bf = A.alloc([KD, NMAX], BF16)
            S.add("dve", I("memset", xcar, 0.0), writes=("xcar",))
            rem = (WIN - out_start) % NMAX
            tl = tiles_of(0, out_start, NMAX) + tiles_of(out_start, rem, NMAX) + \
                tiles_of(out_start + rem, WIN - out_start - rem, NMAX)
            load = tile_loader(src, hbuf, "h", "h", "src", nslots=3)
            SK = ("S", "X", "xcar", "alpha")
            ar, ai = alpha[:, 0, :], alpha[:, 1, :]

            def stage1(i, t0, n):
                hb = hbuf[i % 3]
                hkey = ("h", i % 3)
                u_bf = u_bufs[i % 2]
                Ssb = S_bufs[i % 2]
                UK = "u%d" % (i % 2)
                SKEY = "S%d" % (i % 2)
                C = n // T
                pv = lambda a: a.rearrange("p (c s) -> p c s", s=T)
                rmsnorm(hb, hkey, G_MIX + li, UK, n, sq_t, rstd_t,
                        out_of=lambda k: u_bf[:, k, 0:n].rearrange("p (s c) -> p c s", s=T), view=pv)
                for k in range(KD):
                    for part in range(2):
                        banks = [next_ps() for _ in range(4)]
                        ops = []
                        for s_ in range(T):
                            for qq in range(4):
                                r0 = 32 * qq
                                ops.append(I("matmul", psb[banks[qq]][:, 0:C], SIN[r0:r0 + 32, k, s_, part, :],
                                             u_bf[r0:r0 + 32, k, s_ * C:(s_ + 1) * C],
                                             start=(s_ == 0), stop=(s_ == T - 1), tile_position=(r0, 0)))
                        S.add("pe", ops, reads=("SIN", (UK, k)), writes=tuple(("ps", b_) for b_ in banks))
                        for qq in range(4):
                            S.add("act", I("activation", Ssb[:, part, 4 * k + qq, 0:C], psb[banks[qq]][:, 0:C], AF.Copy),
                                  reads=(("ps", banks[qq]),), writes=(SKEY,))

            def stage2(i, t0, n):
                hb = hbuf[i % 3]
                hkey = ("h", i % 3)
                u_bf = u_bufs[i % 2]
                Ssb = S_bufs[i % 2]
                UK = "u%d" % (i % 2)
                SKEY = "S%d" % (i % 2)
                SK = (SKEY, "X", "xcar", "alpha")
                C = n // T
                full = t0 >= out_start
                S.add("dve", I("tensor_copy", X[:, :, :, 0], xcar), reads=SK, writes=("X",))
                for c in range(C):
                    xr, xi = X[:, 0, :, c], X[:, 1, :, c]
                    S.add("dve", I("tensor_tensor", tA, xr, ar, ALU.mult), reads=("X", "alpha", "xcar"), writes=("tA",))
                    S.add("dve", I("tensor_tensor", tB, xi, ai, ALU.mult), reads=("X", "alpha"), writes=("tB",))
                    S.add("dve", I("tensor_tensor", tC, xi, ar, ALU.mult), reads=("X", "alpha"), writes=("tC",))
                    S.add("dve", I("tensor_tensor", tD, xr, ai, ALU.mult), reads=("X", "alpha"), writes=("tD",))
                    S.add("dve", I("tensor_tensor", tA, tA, tB, ALU.subtract), reads=("tA", "tB"), writes=("tA",))
                    S.add("dve", I("tensor_tensor", tC, tC, tD, ALU.add), reads=("tC", "tD"), writes=("tC",))
                    S.add("dve", I("tensor_tensor", X[:, 1, :, c + 1], tC, Ssb[:, 1, :, c], ALU.add),
                          reads=("tC", SKEY), writes=("X",))
                    S.add("dve", I("tensor_tensor", X[:, 0, :, c + 1], tA, Ssb[:, 0, :, c], ALU.add),
                          reads=("tA", SKEY), writes=("X",))
                S.add("dve", I("tensor_copy", xcar, X[:, :, :, C]), reads=SK, writes=("xcar",))
                if not full:
                    return
                S.add("act", I("activation", X_bf[:, :, :, 0:C], X[:, :, :, 0:C], AF.Copy),
                      reads=("X",), writes=("Xbf",))
                for k in range(KD):
                    b = next_ps()
                    ps = psb[b][:, 0:n]
                    ops = []
                    for d_ in range(T):
                        ops.append(I("matmul", ps[:, d_ * C:T * C], KTt[:, k, d_, :], u_bf[:, k, 0:(T - d_) * C],
                                     start=(d_ == 0), stop=False))
                    for j in range(T):
                        for part in range(2):
                            for qq in range(4):
                                q = 4 * k + qq
                                ops.append(I("matmul", ps[32 * qq:32 * qq + 32, j * C:(j + 1) * C],
                                             OUT[:, q, j, part, :], X_bf[:, part, q, 0:C],
                                             start=False, stop=(j == T - 1 and part == 1),
                                             tile_position=(0, 32 * qq)))
                    S.add("pe", ops, reads=("KT", "OUT", "Xbf", (UK, k)), writes=(("ps", b),))
                    S.add("dve", I("scalar_tensor_tensor", v_t[:, 0:n], u_bf[:, k, 0:n], dvec[:, li, k:k + 1], ps,
                                   ALU.mult, ALU.add), reads=(("ps", b), (UK, k), "dvec"), writes=("v",))
                    S.add("act", I("activation", w_t2[:, 0:n], v_t[:, 0:n], AF.Square), reads=("v",), writes=("w2",))
                    S.add("dve", I("tensor_scalar", w_t2[:, 0:n], w_t2[:, 0:n], 0.044715, 1.0, ALU.mult, ALU.add),
                          reads=("w2",), writes=("w2",))
                    S.add("dve", I("tensor_tensor", w_t2[:, 0:n], w_t2[:, 0:n], v_t[:, 0:n], ALU.mult),
                          reads=("w2", "v"), writes=("w2",))
                    S.add("act", I("activation", sg_t[:, 0:n], w_t2[:, 0:n], AF.Sigmoid, scale=1.5957691216057308),
                          reads=("w2",), writes=("sg",))
                    S.add("dve", I("tensor_tensor", g_bf[:, k, 0:n], v_t[:, 0:n], sg_t[:, 0:n], ALU.mult),
                          reads=("v", "sg"), writes=(("g", k),))
                gkeys = [("g", k) for k in range(KD)]
                for m in range(KD):
                    ba = next_ps(); bb_ = next_ps()
                    psa = psb[ba][:, 0:n]; psg = psb[bb_][:, 0:n]
                    S.add("pe", [I("matmul", psa, wg[:, k, m * 128:(m + 1) * 128], g_bf[:, k, 0:n],
                                   start=(k == 0), stop=(k == KD - 1)) for k in range(KD)] +
                                [I("matmul", psg, wg[:, k, D + m * 128:D + (m + 1) * 128], g_bf[:, k, 0:n],
                                   start=(k == 0), stop=(k == KD - 1)) for k in range(KD)],
                          reads=["wglu"] + gkeys, writes=(("ps", ba), ("ps", bb_)))
                    S.add("act", I("activation", sg_t[:, 0:n], psg, AF.Sigmoid), reads=(("ps", bb_),), writes=("sg",))
                    S.add("dve", I("tensor_tensor", v_t[:, 0:n], psa, sg_t[:, 0:n], ALU.mult),
                          reads=(("ps", ba), "sg"), writes=("v",))
                    hv = hb[:, m, 0:n].rearrange("p (c s) -> p s c", s=T)
                    S.add("dve", I("tensor_tensor", hv, hv, v_t[:, 0:n].rearrange("p (s c) -> p s c", s=T), ALU.add),
                          reads=("v", hkey), writes=(hkey,))
                c0 = t0 - out_start
                S.add("sp", I("dma_start", out=dst[:, c0:c0 + n].rearrange("(k p) n -> p k n", p=128),
                              in_=hb[:, :, 0:n]), reads=(hkey,), writes=(("dst", t0),), chan=f"st{i % 3}")

            for i0 in range(min(2, len(tl))):
                load(i0, tl[i0][0], tl[i0][1], tl[i0][0])
            stage1(0, tl[0][0], tl[0][1])
            for i, (t0, n) in enumerate(tl):
                if i + 2 < len(tl):
                    load(i + 2, tl[i + 2][0], tl[i + 2][1], tl[i + 2][0])
                if i + 1 < len(tl):
                    stage1(i + 1, tl[i + 1][0], tl[i + 1][1])
                stage2(i, t0, n)
            S.barrier()

        def mlp_phase(layer, buf, start, total):
            A.reset(PERSIST)
            ps_pool[0] = list(range(8))
            wu = A.alloc([KD, DFF], BF16)
            wd = A.alloc([DFF // 128, D], BF16)
            for k in range(KD):
                load_w(wu[:, k, :], w_up[layer][k * 128:(k + 1) * 128, :], "wu")
            for k4 in range(0, DFF // 128, 8):
                load_w(wd[:, k4:k4 + 8, :],
                       w_down[layer][k4 * 128:(k4 + 8) * 128, :].rearrange("(k p) o -> p k o", p=128), "wd")
            NM = 256
            hbuf = [A.alloc([KD, NM], F32) for _ in range(2)]
            sq_t = A.alloc([KD, NM], BF16)
            rstd_t = A.alloc([NM], F32)
            hm = A.alloc([KD, NM], BF16)
            ff = A.alloc([DFF // 128, NM], BF16)
            r_t = [A.alloc([NM], F32) for _ in range(2)]
            tl = tiles_of(start, total % NM, NM) + tiles_of(start + total % NM, total - total % NM, NM)
            load = tile_loader(buf, hbuf, "h", "h", "buf")

            def do_tile(i, t0, n):
                hb = hbuf[i % 2]
                hkey = ("h", i % 2)
                rmsnorm(hb, hkey, G_MLP + layer, "hm", n, sq_t, rstd_t, out_of=lambda k: hm[:, k, 0:n])
                hmk = [("hm", k) for k in range(KD)]

                def ev_up(m, b, ps):
                    rt = r_t[m % 2]
                    S.add("act", I("activation", rt[:, 0:n], ps, AF.Relu), reads=(("ps", b),), writes=(("r", m % 2),))
                    S.add("dve", I("tensor_tensor", ff[:, m, 0:n], rt[:, 0:n], rt[:, 0:n], ALU.mult),
                          reads=(("r", m % 2),), writes=(("ff", m),))
                gemm(wu, "wu", KD, lambda k: hm[:, k, 0:n], hmk, range(DFF // 128), n, ev_up)
                ffk = [("ff", m) for m in range(DFF // 128)]

                def ev_dn(m, b, ps):
                    S.add("dve", I("tensor_tensor", hb[:, m, 0:n], hb[:, m, 0:n], ps, ALU.add),
                          reads=(("ps", b), hkey), writes=(hkey,))
                gemm(wd, "wd", DFF // 128, lambda k: ff[:, k, 0:n], ffk, range(KD), n, ev_dn)
                S.add("sp", I("dma_start", out=buf[:, t0:t0 + n].rearrange("(k p) n -> p k n", p=128),
                              in_=hb[:, :, 0:n]), reads=(hkey,), writes=(("buf", t0),), chan=f"st{i % 2}")

            load(0, tl[0][0], tl[0][1], tl[0][0])
            for i, (t0, n) in enumerate(tl):
                if i + 1 < len(tl):
                    load(i + 1, tl[i + 1][0], tl[i + 1][1], tl[i + 1][0])
                do_tile(i, t0, n)
            S.barrier()

        def ple_phase(layer, buf, start, total, p_src, do_kv=False, final=False):
            A.reset(PERSIST)
            ps_pool[0] = list(range(8))
            wgt = A.alloc([KD, D], BF16)
            wpj = A.alloc([2, D], BF16)
            load_w(wgt, w_gate[layer].rearrange("(k p) o -> p k o", p=128), "wgt")
            load_w(wpj, w_proj[layer].rearrange("(k p) o -> p k o", p=128), "wpj")
            NM = 512
            if do_kv:
                wk2 = A.alloc([KD, NKV, 128], BF16)
                wks2 = A.alloc([KD, NKV, 128], BF16)
                wv = A.alloc([KD, 256], BF16)
                for dup in range(2):
                    for k in range(KD):
                        load_w(wk2[:, k, :, 64 * dup:64 * dup + 64],
                               w_k[k * 128:(k + 1) * 128, :].rearrange("p (v d) -> p v d", d=64), "wk2")
                        load_w(wks2[:, k, :, 64 * dup:64 * dup + 64],
                               w_ks[k * 128:(k + 1) * 128, :].rearrange("p (v d) -> p v d", d=64), "wks2")
                load_w(wv, w_v.rearrange("(k p) o -> p k o", p=128), "wv")
                cs_t = A.alloc([2, NM], F32)
                kt_t = A.alloc([NKV, NM], BF16)
                vt_t = A.alloc([NM // 128, 256], BF16)
            hbuf = [A.alloc([KD, NM], F32) for _ in range(2)]
            pbuf = [A.alloc([2, NM], F32) for _ in range(2)]
            p_bf = A.alloc([2, NM], BF16)
            sq_t = A.alloc([KD, NM], BF16)
            rstd_t = A.alloc([NM], F32)
            hp = A.alloc([KD, NM], BF16)
            gate_t = A.alloc([NM], F32)
            tmp_t = A.alloc([NM], F32)
            tl = tiles_of(start, total % NM, NM) + tiles_of(start + total % NM, total - total % NM, NM)
            loadh = tile_loader(buf, hbuf, "h", "h", "buf")
            loadp = tile_loader(p_src, pbuf, "p", "p", "psrc")

            def load(i):
                t0, n = tl[i]
                loadh(i, t0, n, t0)
                loadp(i, t0, n, t0 - start)

            def do_tile(i, t0, n):
                hb = hbuf[i % 2]
                hkey = ("h", i % 2)
                pb = pbuf[i % 2]
                S.add("act", I("activation", p_bf[:, :, 0:n], pb[:, :, 0:n], AF.Copy),
                      reads=(("p", i % 2),), writes=("pbf",))
                rmsnorm(hb, hkey, G_PLE + layer, "hp", n, sq_t, rstd_t, out_of=lambda k: hp[:, k, 0:n])
                hpk = [("hp", k) for k in range(KD)]
                for m in range(KD):
                    bg = next_ps(); bp = next_ps()
                    psg = psb[bg][:, 0:n]; psp = psb[bp][:, 0:n]
                    S.add("pe", [I("matmul", psg, wgt[:, k, m * 128:(m + 1) * 128], hp[:, k, 0:n],
                                   start=(k == 0), stop=(k == KD - 1)) for k in range(KD)] +
                                [I("matmul", psp, wpj[:, 0, m * 128:(m + 1) * 128], p_bf[:, 0, 0:n], start=True, stop=False),
                                 I("matmul", psp, wpj[:, 1, m * 128:(m + 1) * 128], p_bf[:, 1, 0:n], start=False, stop=True)],
                          reads=["wgt", "wpj", "pbf"] + hpk, writes=(("ps", bg), ("ps", bp)))
                    S.add("act", I("activation", gate_t[:, 0:n], psg, AF.Sigmoid), reads=(("ps", bg),), writes=("gate",))
                    S.add("dve", I("tensor_tensor", tmp_t[:, 0:n], psp, gate_t[:, 0:n], ALU.mult),
                          reads=(("ps", bp), "gate"), writes=("tmp",))
                    S.add("dve", I("tensor_tensor", hb[:, m, 0:n], hb[:, m, 0:n], tmp_t[:, 0:n], ALU.add),
                          reads=("tmp", hkey), writes=(hkey,))
                if final:
                    rmsnorm(hb, hkey, G_FIN, "fin", n, sq_t, rstd_t, out_of=lambda k: hb[:, k, 0:n])
                    S.add("sp", I("dma_start", out=outT[:, t0 - start:t0 - start + n].rearrange("(k p) n -> p k n", p=128),
                                  in_=hb[:, :, 0:n]), reads=[hkey] + [("fin", k) for k in range(KD)],
                          writes=(("out", t0),), chan=f"st{i % 2}")
                else:
                    S.add("sp", I("dma_start", out=buf[:, t0:t0 + n].rearrange("(k p) n -> p k n", p=128),
                                  in_=hb[:, :, 0:n]), reads=(hkey,), writes=(("buf", t0),), chan=f"st{i % 2}")
                if do_kv:
                    a0 = t0 - start
                    S.add("sp", I("dma_start", out=cs_t[:, 0, 0:n], in_=cosT[:, a0:a0 + n]), writes=("cs",), chan="cs")
                    S.add("sp", I("dma_start", out=cs_t[:, 1, 0:n], in_=sinT[:, a0:a0 + n]), writes=("cs",), chan="cs")
                    rmsnorm(hb, hkey, G_KV, "hk", n, sq_t, rstd_t, out_of=lambda k: hp[:, k, 0:n])
                    hkk = [("hk", k) for k in range(KD)]
                    for kv in range(NKV):
                        b1 = next_ps(); b2 = next_ps()
                        ps1 = psb[b1][:, 0:n]; ps2 = psb[b2][:, 0:n]
                        S.add("pe", [I("matmul", ps1, wk2[:, k, kv, :], hp[:, k, 0:n], start=(k == 0), stop=(k == KD - 1))
                                     for k in range(KD)] +
                                    [I("matmul", ps2, wks2[:, k, kv, :], hp[:, k, 0:n], start=(k == 0), stop=(k == KD - 1))
                                     for k in range(KD)],
                              reads=["wk2", "wks2"] + hkk, writes=(("ps", b1), ("ps", b2)))
                        S.add("dve", I("tensor_tensor", gate_t[:, 0:n], ps1, cs_t[:, 0, 0:n], ALU.mult),
                              reads=(("ps", b1), "cs"), writes=("gate",))
                        S.add("dve", I("tensor_tensor", tmp_t[:, 0:n], ps2, cs_t[:, 1, 0:n], ALU.mult),
                              reads=(("ps", b2), "cs"), writes=("tmp",))
                        S.add("dve", I("tensor_tensor", kt_t[:, kv, 0:n], gate_t[:, 0:n], tmp_t[:, 0:n], ALU.add),
                              reads=("gate", "tmp"), writes=("ktt",))
                    S.add("sp", I("dma_start", out=kT2_d[:, :, a0:a0 + n], in_=kt_t[:, :, 0:n]),
                          reads=("ktt",), writes=("kT2d",), chan="kst")
                    for blk in range(n // 128):
                        b = next_ps()
                        ps = psb[b][:, 0:256]
                        S.add("pe", [I("matmul", ps, hp[:, k, blk * 128:(blk + 1) * 128], wv[:, k, :],
                                       start=(k == 0), stop=(k == KD - 1)) for k in range(KD)],
                              reads=["wv"] + hkk, writes=(("ps", b),))
                        S.add("act", I("activation", vt_t[:, blk, :], ps, AF.Copy), reads=(("ps", b),), writes=("vtt",))
                    S.add("sp", I("dma_start", out=v_d[:, a0 // 128:a0 // 128 + n // 128, :], in_=vt_t[:, 0:n // 128, :]),
                          reads=("vtt",), writes=("vd",), chan="vst")

            load(0)
            for i, (t0, n) in enumerate(tl):
                if i + 1 < len(tl):
                    load(i + 1)
                do_tile(i, t0, n)
            S.barrier()

        def attn_phase(layer):
            j = layer - 2
            A.reset(PERSIST)
            ps_pool[0] = list(range(6))
            wq = A.alloc([KD, D], BF16)
            wqs = A.alloc([KD, D], BF16)
            wo = A.alloc([KD, D], BF16)
            load_w(wq, w_q[j].rearrange("(k p) o -> p k o", p=128), "wq")
            load_w(wqs, w_qs[j].rearrange("(k p) o -> p k o", p=128), "wqs")
            load_w(wo, w_o[j].rearrange("(k p) o -> p k o", p=128), "wo")
            kT2 = A.alloc([NKV, OWNH], BF16)
            v_bf = A.alloc([NBLK, 256], BF16)
            S.add("sp", I("dma_start", out=kT2, in_=kT2_d), writes=("kT2",), chan="kld")
            S.add("sp", I("dma_start", out=v_bf, in_=v_d), writes=("vbf",), chan="kld")
            NM = 512
            hbuf = [A.alloc([KD, NM], F32) for _ in range(2)]
            cs_t = A.alloc([2, NM], F32)
            sq_t = A.alloc([KD, NM], BF16)
            rstd_t = A.alloc([NM], F32)
            hn = A.alloc([KD, NM], BF16)
            qT = A.alloc([KD, NM], BF16)
            t1 = A.alloc([NM], F32); t2 = A.alloc([NM], F32)
            sm_l = [A.alloc([2, 256], F32) for _ in range(2)]
            e_l = [A.alloc([2, 256], BF16) for _ in range(2)]
            eT_l = [A.alloc([2, 2, 128], BF16) for _ in range(2)]
            stat_l = [A.alloc([8], F32) for _ in range(2)]
            rinv = A.alloc([NH], F32)
            o_bf = A.alloc([D], BF16)
            oT = A.alloc([KD, NM], BF16)
            tl = tiles_of(HALO, OWN, NM)
            load = tile_loader(hB, hbuf, "h", "h", "buf")

            def do_tile(i, t0, n):
                hb = hbuf[i % 2]
                hkey = ("h", i % 2)
                S.add("sp", I("dma_start", out=cs_t[:, 0, 0:n], in_=cosT[:, t0:t0 + n]), writes=("cs",), chan="cs")
                S.add("sp", I("dma_start", out=cs_t[:, 1, 0:n], in_=sinT[:, t0:t0 + n]), writes=("cs",), chan="cs")
                rmsnorm(hb, hkey, G_MIX + layer, "hn", n, sq_t, rstd_t, out_of=lambda k: hn[:, k, 0:n])
                hnk = [("hn", k) for k in range(KD)]
                for m in range(KD):
                    b1 = next_ps(); b2 = next_ps()
                    ps1 = psb[b1][:, 0:n]; ps2 = psb[b2][:, 0:n]
                    S.add("pe", [I("matmul", ps1, wq[:, k, m * 128:(m + 1) * 128], hn[:, k, 0:n],
                                   start=(k == 0), stop=(k == KD - 1)) for k in range(KD)] +
                                [I("matmul", ps2, wqs[:, k, m * 128:(m + 1) * 128], hn[:, k, 0:n],
                                   start=(k == 0), stop=(k == KD - 1)) for k in range(KD)],
                          reads=["wq", "wqs"] + hnk, writes=(("ps", b1), ("ps", b2)))
                    S.add("dve", I("tensor_tensor", t1[:, 0:n], ps1, cs_t[:, 0, 0:n], ALU.mult),
                          reads=(("ps", b1), "cs"), writes=("t1",))
                    S.add("dve", I("tensor_tensor", t2[:, 0:n], ps2, cs_t[:, 1, 0:n], ALU.mult),
                          reads=(("ps", b2), "cs"), writes=("t2",))
                    S.add("dve", I("tensor_tensor", qT[:, m, 0:n], t1[:, 0:n], t2[:, 0:n], ALU.add),
                          reads=("t1", "t2"), writes=(("q", m),))
                for blk in range(n // 128):
                    ab = (t0 - HALO) // 128 + blk
                    msk = mask0_t if ab == 0 else maskG_t
                    mkey = "mask0" if ab == 0 else "maskG"
                    bo = [6, 7]

                    def qk_softmax(hp_, blk=blk, ab=ab, msk=msk, mkey=mkey):
                        par = hp_ % 2
                        sm, e_bf, stat = sm_l[par], e_l[par], stat_l[par]
                        bsl = [next_ps(), next_ps()]
                        for a in range(2):
                            S.add("pe", I("matmul", psb[bsl[a]][:, 0:256],
                                          qT[64 * a:64 * a + 64, hp_, blk * 128:(blk + 1) * 128],
                                          kT2[64 * a:64 * a + 64, (2 * hp_ + a) // 4, ab * 128:ab * 128 + 256],
                                          start=True, stop=True, tile_position=(64 * a, 0)),
                                  reads=(("q", hp_), "kT2"), writes=(("ps", bsl[a]),))
                        for a in range(2):
                            hd = 2 * hp_ + a
                            S.add("dve", I("scalar_tensor_tensor", sm[:, a, :], psb[bsl[a]][:, 0:256], 0.125, msk,
                                           ALU.mult, ALU.add),
                                  reads=(("ps", bsl[a]), mkey), writes=(("sm", par, a),))
                            S.add("dve", I("reduce_max", stat[:, a:a + 1], sm[:, a, :], axis=AX.X),
                                  reads=(("sm", par, a),), writes=(("mx", par, a),))
                            S.add("dve", I("tensor_scalar", stat[:, 2 + a:3 + a], stat[:, a:a + 1],
                                           sink_t[:, j, hd:hd + 1], -1.0, ALU.max, ALU.mult),
                                  reads=(("mx", par, a), "sink"), writes=(("ngm", par, a),))
                            S.add("dve", I("memset", stat[:, 4 + a:5 + a], 0.0), writes=(("rs", par, a),))
                            S.add("act", I("activation", e_bf[:, a, :], sm[:, a, :], AF.Exp, bias=stat[:, 2 + a:3 + a],
                                           scale=1.0, accum_out=stat[:, 4 + a:5 + a]),
                                  reads=(("sm", par, a), ("ngm", par, a)), writes=(("e", par, a), ("rs", par, a)))
                            S.add("act", I("activation", stat[:, 6 + a:7 + a], stat[:, 2 + a:3 + a], AF.Exp,
                                           bias=sink_t[:, j, hd:hd + 1], scale=1.0),
                                  reads=(("ngm", par, a), "sink"), writes=(("es", par, a),))
                            S.add("dve", I("tensor_tensor", stat[:, 4 + a:5 + a], stat[:, 4 + a:5 + a],
                                           stat[:, 6 + a:7 + a], ALU.add),
                                  reads=(("rs", par, a), ("es", par, a)), writes=(("rs", par, a),))
                            S.add("dve", I("reciprocal", rinv[:, hd:hd + 1], stat[:, 4 + a:5 + a]),
                                  reads=(("rs", par, a),), writes=(("rinv", hd),))

                    def tr_pv(hp_, ab=ab):
                        par = hp_ % 2
                        e_bf, eT = e_l[par], eT_l[par]
                        bt = next_ps()
                        pst = psb[bt][:].bitcast(BF16)[:, 0:512].rearrange("p (a c q) -> p a c q", a=2, c=2)
                        S.add("pe", [I("transpose", pst[:, a, c, :], e_bf[:, a, c * 128:(c + 1) * 128], ident_bf)
                                     for a in range(2) for c in range(2)],
                              reads=(("e", par, 0), ("e", par, 1), "identbf"), writes=(("ps", bt),))
                        S.add("act", I("activation", eT, pst, AF.Copy), reads=(("ps", bt),), writes=(("eT", par),))
                        pso = psb[bo[hp_ // 4]][:, 0:512].rearrange("p (h d) -> p h d", d=64)
                        S.add("pe", [I("matmul", pso[:, (2 * hp_ + a) % 8, :], eT[:, a, c, :],
                                       v_bf[:, ab + c, ((2 * hp_ + a) // 4) * 64:((2 * hp_ + a) // 4 + 1) * 64],
                                       start=(c == 0), stop=(c == 1)) for a in range(2) for c in range(2)],
                              reads=(("eT", par), "vbf"), writes=(("pso", hp_),))

                    qk_softmax(0)
                    for hp_ in range(NH // 2):
                        if hp_ + 1 < NH // 2:
                            qk_softmax(hp_ + 1)
                        tr_pv(hp_)
                    for half in range(2):
                        pso = psb[bo[half]][:, 0:512].rearrange("p (h d) -> p h d", d=64)
                        ov = o_bf[:, half * 512:(half + 1) * 512].rearrange("p (h d) -> p h d", d=64)
                        S.add("dve", I("tensor_tensor", ov, pso,
                                       rinv[:, half * 8:half * 8 + 8].unsqueeze(2).to_broadcast([128, 8, 64]), ALU.mult),
                              reads=[("pso", hp_) for hp_ in range(4 * half, 4 * half + 4)] +
                                    [("rinv", hd) for hd in range(8 * half, 8 * half + 8)],
                              writes=(("ob", half),) + tuple(("pso", hp_) for hp_ in range(4 * half, 4 * half + 4)))
                    for kk in range(KD):
                        bt = next_ps()
                        pst = psb[bt][:].bitcast(BF16)[:, 0:128]
                        S.add("pe", I("transpose", pst, o_bf[:, kk * 128:(kk + 1) * 128], ident_bf),
                              reads=(("ob", kk // 4), "identbf"), writes=(("ps", bt),))
                        S.add("act", I("activation", oT[:, kk, blk * 128:(blk + 1) * 128], pst, AF.Copy),
                              reads=(("ps", bt),), writes=(("oT", kk),))
                oTk = [("oT", kk) for kk in range(KD)]

                def ev_o(m, b, ps):
                    S.add("dve", I("tensor_tensor", hb[:, m, 0:n], hb[:, m, 0:n], ps, ALU.add),
                          reads=(("ps", b), hkey), writes=(hkey,))
                gemm(wo, "wo", KD, lambda k: oT[:, k, 0:n], oTk, range(KD), n, ev_o)
                S.add("sp", I("dma_start", out=hB[:, t0:t0 + n].rearrange("(k p) n -> p k n", p=128),
                              in_=hb[:, :, 0:n]), reads=(hkey,), writes=(("buf", t0),), chan=f"st{i % 2}")

            load(0, tl[0][0], tl[0][1], tl[0][0])
            for i, (t0, n) in enumerate(tl):
                if i + 1 < len(tl):
                    load(i + 1, tl[i + 1][0], tl[i + 1][1], tl[i + 1][0])
                do_tile(i, t0, n)
            S.barrier()

        S.barrier()
        stages = STAGES
        if stages >= 1:
            s5_phase(0, xT, hA, 0)
        if stages >= 2:
            mlp_phase(0, hA, 0, WIN)
        if stages >= 3:
            ple_phase(0, hA, 0, WIN, p0T)
        if stages >= 4:
            s5_phase(1, hA, hB, PRE)
        if stages >= 5:
            mlp_phase(1, hB, 0, OWNH)
        if stages >= 6:
            ple_phase(1, hB, 0, OWNH, pLT[0], do_kv=True)
        if stages >= 7:
            attn_phase(2)
        if stages >= 8:
            mlp_phase(2, hB, HALO, OWN)
        if stages >= 9:
            ple_phase(2, hB, HALO, OWN, pLT[1][:, HALO:OWNH])
        if stages >= 10:
            attn_phase(3)
        if stages >= 11:
            mlp_phase(3, hB, HALO, OWN)
        if stages >= 12:
            ple_phase(3, hB, HALO, OWN, pLT[2][:, HALO:OWNH], final=True)
        if stages < 12:
            srcb = hA[:, WIN - OWN:WIN] if stages <= 3 else hB[:, HALO:OWNH]
            S.add("sp", I("dma_start", out=outT, in_=srcb), writes=("dbg",), chan="dbg")
        S.finish()
    _DBG_MAPS[(WIN, OWN)] = dbg_map
    return nc


STAGES = 12
ATT_LEVEL = 6
DBGW = 0
PREP_ONLY = False
_DBG_MAPS = {}


def _rope_tables(pos):
    inv = ROPE_THETA ** (-np.arange(0, ROT, 2, dtype=np.float32) / ROT)
    ang = pos.astype(np.float32)[None, :] * inv[:, None].astype(np.float32)
    cos = np.cos(ang).astype(np.float32)
    sin = np.sin(ang).astype(np.float32)
    n = pos.shape[0]
    C = np.ones((64, n), np.float32)
    Sg = np.zeros((64, n), np.float32)
    C[0:8] = cos; C[8:16] = cos
    Sg[0:8] = -sin; Sg[8:16] = sin
    return np.concatenate([C, C], 0), np.concatenate([Sg, Sg], 0)


def _swap_cols(w, nheads):
    w = w.reshape(w.shape[0], nheads, HD)
    ws = w.copy()
    ws[:, :, 0:8] = w[:, :, 8:16]
    ws[:, :, 8:16] = w[:, :, 0:8]
    return np.ascontiguousarray(ws.reshape(w.shape[0], nheads * HD))


def _masks():
    qi = np.arange(128)[:, None] + 128
    kj = np.arange(256)[None, :]
    band = (kj <= qi) & (qi - kj < 128)
    mg = np.where(band, 0.0, -30000.0).astype(np.float32)
    m0 = np.where(band & (kj >= 128), 0.0, -30000.0).astype(np.float32)
    return m0, mg


_PROG_CACHE = {}


def run_config(inputs, NQ, n_cores=8):
    x = np.asarray(inputs["x"], np.float32)
    p = np.asarray(inputs["p"], np.float32)
    Bsz, SEQ, _ = x.shape
    OWN = SEQ // NQ
    WIN = SEQ
    HALO = 128
    OWNH = OWN + HALO
    key = (WIN, OWN)
    if key not in _PROG_CACHE:
        _PROG_CACHE[key] = build_program(WIN, OWN, HALO)
    nc = _PROG_CACHE[key]
    m0, mg = _masks()
    shared = {}
    for name in ("norm_mix", "ssm_lambda_re", "ssm_lambda_im", "ssm_log_dt", "ssm_b_re", "ssm_b_im",
                 "ssm_c_re", "ssm_c_im", "ssm_d", "ssm_w_glu", "kv_norm", "w_k", "w_v", "w_q",
                 "attn_sinks", "w_o", "norm_mlp", "w_up", "w_down", "norm_ple", "w_ple_gate",
                 "w_ple_proj", "norm_final"):
        shared[name] = np.ascontiguousarray(np.asarray(inputs[name], np.float32))
    shared["w_ks"] = _swap_cols(shared["w_k"], NKV)
    shared["w_qs"] = np.stack([_swap_cols(shared["w_q"][i], NH) for i in range(2)])
    shared["ident"] = np.eye(128, dtype=np.float32)
    bd = np.zeros((128, 128), np.float32)
    for g in range(8):
        bd[16 * g:16 * g + 16, 16 * g:16 * g + 16] = 1.0
    shared["bdmask"] = bd
    shared["maskG"] = mg
    in_maps = []
    cores = []
    for c in range(n_cores):
        cc = c % (Bsz * NQ)
        b, q = cc // NQ, cc % NQ
        cores.append((b, q))
        own_end = (q + 1) * OWN
        w0 = own_end - WIN
        xw = np.zeros((WIN, D), np.float32)
        pw = np.zeros((WIN, PLE), np.float32)
        lo = max(w0, 0)
        xw[lo - w0:] = x[b, lo:own_end]
        pw[lo - w0:] = p[0, b, lo:own_end]
        h0 = q * OWN - HALO
        pl = np.zeros((3, OWNH, PLE), np.float32)
        lo2 = max(h0, 0)
        pl[:, lo2 - h0:] = p[1:4, b, lo2:own_end]
        pos = np.arange(h0, own_end)
        cosT, sinT = _rope_tables(pos)
        m = dict(shared)
        m["xT"] = np.ascontiguousarray(xw.T)
        m["p0T"] = np.ascontiguousarray(pw.T)
        m["pLT"] = np.ascontiguousarray(pl.transpose(0, 2, 1))
        m["cosT"] = cosT
        m["sinT"] = sinT
        m["mask0"] = m0 if q == 0 else mg
        in_maps.append(m)
    if PREP_ONLY:
        return nc, in_maps
    res = run_bass_kernel_spmd(nc, in_maps, core_ids=list(range(n_cores)))
    if DBGW:
        global LAST_DBG
        LAST_DBG = ([r["dbg"] for r in res.results], _DBG_MAPS[key])
    out = np.zeros((Bsz, SEQ, D), np.float32)
    for c in range(min(n_cores, Bsz * NQ)):
        b, q = cores[c]
        out[b, q * OWN:(q + 1) * OWN] = res.results[c]["outT"].T
    return out


def kernel(**inputs):
    return run_config(inputs, NQ=4, n_cores=8)
```

```python
import math
from contextlib import ExitStack

import numpy as np
import concourse.bass as bass
import concourse.mybir as mybir
from concourse.bass_utils import run_bass_kernel_spmd

F32 = mybir.dt.float32
BF16 = mybir.dt.bfloat16
AF = mybir.ActivationFunctionType
ALU = mybir.AluOpType
AX = mybir.AxisListType

D = 1024
KD = 8
DFF = 4096
PLE = 256
NH = 16
NKV = 4
HD = 64
T = 8
NPAIR = 32
EPS = 1e-6
ROPE_THETA = 500000.0
ROT = 16

SAME_ENGINE_SYNC = True


class Sched:
    ENG = ("pe", "act", "dve", "pool", "sp")

    def __init__(self, nc, stack):
        self.nc = nc
        self.stack = stack
        self.prog = {e: [] for e in self.ENG}
        self.sem = {}
        self.count = {}
        self.unit = {}
        self.last_write = {}
        self.readers = {}
        self.waited = {e: {} for e in self.ENG}
        for e in self.ENG:
            self._stream(e, 1)

    def _stream(self, name, unit):
        if name not in self.sem:
            self.sem[name] = self.stack.enter_context(self.nc.semaphore("s_" + name.replace(":", "_")))
            self.count[name] = 0
            self.unit[name] = unit
        return name

    def _wait(self, engine, stream, value):
        if self.waited[engine].get(stream, 0) >= value:
            return
        self.waited[engine][stream] = value
        sem = self.sem[stream]
        self.prog[engine].append(lambda eng, sem=sem, value=value: eng.wait_ge(sem, value))

    def add(self, engine, emit, reads=(), writes=(), chan=None):
        stream = engine if chan is None else self._stream("dma:" + chan, 16)
        deps = {}
        for k in reads:
            lw = self.last_write.get(k)
            if lw is not None:
                deps[lw[0]] = max(deps.get(lw[0], 0), lw[1] + 1)
        for k in writes:
            lw = self.last_write.get(k)
            if lw is not None:
                deps[lw[0]] = max(deps.get(lw[0], 0), lw[1] + 1)
            for r in self.readers.get(k, ()):
                deps[r[0]] = max(deps.get(r[0], 0), r[1] + 1)
        if chan is not None and self.count[stream] > 0:
            deps[stream] = max(deps.get(stream, 0), self.count[stream])
        for s, n in deps.items():
            if s == engine and chan is None:
                if engine == "pe" or not SAME_ENGINE_SYNC:
                    continue
            self._wait(engine, s, n * self.unit[s])
        sem = self.sem[stream]
        unit = self.unit[stream]
        ops = [emit] if isinstance(emit, tuple) else list(emit)
        assert ops and all(isinstance(o, tuple) for o in ops)

        def run(eng, ops=ops, sem=sem, unit=unit):
            last = None
            for (meth, args, kw) in ops:
                last = getattr(eng, meth)(*args, **kw)
            last.then_inc(sem, unit)
        self.prog[engine].append(run)
        idx = self.count[stream]
        self.count[stream] = idx + 1
        for k in writes:
            self.last_write[k] = (stream, idx)
            self.readers[k] = []
        for k in reads:
            self.readers.setdefault(k, []).append((stream, idx))

    def barrier(self):
        for e in self.ENG:
            for s in self.sem:
                if s != e and self.count[s] > 0:
                    self._wait(e, s, self.count[s] * self.unit[s])
        self.last_write.clear()
        self.readers.clear()

    def finish(self):
        self.barrier()
        nc = self.nc
        with nc.Block() as block:
            @block.tensor
            def _(eng):
                for f in self.prog["pe"]:
                    f(eng)

            @block.scalar
            def _(eng):
                for f in self.prog["act"]:
                    f(eng)

            @block.vector
            def _(eng):
                for f in self.prog["dve"]:
                    f(eng)

            @block.gpsimd
            def _(eng):
                for f in self.prog["pool"]:
                    f(eng)

            @block.sync
            def _(eng):
                for f in self.prog["sp"]:
                    f(eng)


def I(meth, *args, **kw):
    return (meth, args, kw)


class Arena:
    def __init__(self, ap, words):
        self.ap = ap
        self.words = words
        self.off = 0

    def reset(self, to=0):
        self.off = to

    def alloc(self, free_shape, dtype):
        n = 1
        for s in free_shape:
            n *= s
        w = n if dtype == F32 else (n + 1) // 2
        w = (w + 7) // 8 * 8
        assert self.off + w <= self.words, f"SBUF arena overflow {self.off}+{w}>{self.words}"
        v = self.ap[:, self.off:self.off + w]
        self.off += w
        if dtype != F32:
            v = v.bitcast(dtype)
        v = v[:, 0:n]
        if len(free_shape) == 1:
            return v
        names = " ".join(f"a{i}" for i in range(len(free_shape)))
        kw = {f"a{i}": s for i, s in enumerate(free_shape[:-1])}
        return v.rearrange(f"p ({names}) -> p {names}", **kw)


def tiles_of(start, total, size):
    out = []
    t = start
    end = start + total
    while t < end:
        n = min(size, end - t)
        out.append((t, n))
        t += n
    return out


def build_program(WIN, OWN, HALO=128, ARENA_WORDS=50 * 1024):
    OWNH = OWN + HALO
    PRE = WIN - OWNH
    assert PRE >= 0 and PRE % 128 == 0 and OWN % 512 == 0
    NBLK = OWNH // 128
    nc = bass.Bass("TRN2", target_bir_lowering=False)
    dt_in = lambda name, shape: nc.dram_tensor(name, list(shape), F32, kind="ExternalInput").ap()
    xT = dt_in("xT", [D, WIN])
    p0T = dt_in("p0T", [PLE, WIN])
    pLT = dt_in("pLT", [3, PLE, OWNH])
    cosT = dt_in("cosT", [128, OWNH])
    sinT = dt_in("sinT", [128, OWNH])
    mask0 = dt_in("mask0", [128, 256])
    maskG = dt_in("maskG", [128, 256])
    ident_d = dt_in("ident", [128, 128])
    bdmask_d = dt_in("bdmask", [128, 128])
    norm_mix = dt_in("norm_mix", [4, D])
    lam_re = dt_in("ssm_lambda_re", [2, 64, 64])
    lam_im = dt_in("ssm_lambda_im", [2, 64, 64])
    log_dt = dt_in("ssm_log_dt", [2, 64])
    b_re = dt_in("ssm_b_re", [2, 64, 64, 16])
    b_im = dt_in("ssm_b_im", [2, 64, 64, 16])
    c_re = dt_in("ssm_c_re", [2, 64, 16, 64])
    c_im = dt_in("ssm_c_im", [2, 64, 16, 64])
    ssm_d = dt_in("ssm_d", [2, D])
    w_glu = dt_in("ssm_w_glu", [2, D, 2 * D])
    kv_norm = dt_in("kv_norm", [D])
    w_k = dt_in("w_k", [D, 256])
    w_ks = dt_in("w_ks", [D, 256])
    w_v = dt_in("w_v", [D, 256])
    w_q = dt_in("w_q", [2, D, D])
    w_qs = dt_in("w_qs", [2, D, D])
    sinks = dt_in("attn_sinks", [2, NH])
    w_o = dt_in("w_o", [2, D, D])
    norm_mlp = dt_in("norm_mlp", [4, D])
    w_up = dt_in("w_up", [4, D, DFF])
    w_down = dt_in("w_down", [4, DFF, D])
    norm_ple = dt_in("norm_ple", [4, D])
    w_gate = dt_in("w_ple_gate", [4, D, D])
    w_proj = dt_in("w_ple_proj", [4, PLE, D])
    norm_final = dt_in("norm_final", [D])
    outT = nc.dram_tensor("outT", [D, OWN], F32, kind="ExternalOutput").ap()
    hA = nc.dram_tensor("hA", [D, WIN], F32, kind="Internal").ap()
    hB = nc.dram_tensor("hB", [D, OWNH], F32, kind="Internal").ap()
    kT2_d = nc.dram_tensor("kT2_d", [128, NKV, OWNH], BF16, kind="Internal").ap()
    dbg_d = nc.dram_tensor("dbg", [128, DBGW], F32, kind="ExternalOutput").ap() if DBGW else None
    dbg_off = [0]
    dbg_map = {}
    v_d = nc.dram_tensor("v_d", [128, NBLK, 256], BF16, kind="Internal").ap()

    stack = ExitStack()
    with stack:
        S = Sched(nc, stack)
        arena_t = stack.enter_context(nc.sbuf_tensor("arena", [128, ARENA_WORDS], F32))
        A = Arena(arena_t[:], ARENA_WORDS)
        psb = [stack.enter_context(nc.psum_tensor(f"ps{i}", [128, 512], F32)) for i in range(8)]
        ps_pool = [list(range(8))]
        ps_rr = [0]

        def next_ps():
            pool = ps_pool[0]
            i = pool[ps_rr[0] % len(pool)]
            ps_rr[0] += 1
            return i

        ident = A.alloc([128], F32)
        ident_bf = A.alloc([128], BF16)
        bdmask = A.alloc([128], F32)
        ones_bf = A.alloc([128], BF16)
        gam = A.alloc([14, KD], F32)
        dvec = A.alloc([2, KD], F32)
        sink_t = A.alloc([2, NH], F32)
        mask0_t = A.alloc([256], F32)
        maskG_t = A.alloc([256], F32)
        xcar = A.alloc([2, NPAIR], F32)
        eps_t = A.alloc([1], F32)
        hpi_t = A.alloc([1], F32)
        PERSIST = A.off

        def dump(name, ap, keys):
            if dbg_d is None or name in dbg_map:
                return
            w = 1
            for d_ in ap.shape[1:]:
                w *= d_
            if dbg_off[0] + w > DBGW:
                return
            dst = dbg_d[:, dbg_off[0]:dbg_off[0] + w]
            if len(ap.shape) > 2:
                names = " ".join(f"a{i}" for i in range(len(ap.shape) - 1))
                kw = {f"a{i}": d_ for i, d_ in enumerate(ap.shape[1:-1])}
                dst = dst.rearrange(f"p ({names}) -> p {names}", **kw)
            dbg_map[name] = (dbg_off[0], tuple(ap.shape[1:]))
            dbg_off[0] += w
            S.add("sp" if ap.dtype == F32 else "pool", I("dma_start", out=dst, in_=ap), reads=tuple(keys),
                  writes=(("dbg", name),), chan="dbgc" if ap.dtype == F32 else "dbgp")

        def dma_in(eng, out, in_, key, chan, **kw):
            S.add(eng, I("dma_start", out=out, in_=in_, **kw), reads=(), writes=(key,), chan=chan)

        dma_in("sp", ident, ident_d, "ident", "c0")
        dma_in("sp", bdmask, bdmask_d, "bdmask", "c0")
        dma_in("sp", mask0_t, mask0, "mask0", "c0")
        dma_in("sp", maskG_t, maskG, "maskG", "c0")
        gsrc = [norm_mix[i] for i in range(4)] + [norm_mlp[i] for i in range(4)] + \
               [norm_ple[i] for i in range(4)] + [kv_norm, norm_final]
        for i, g in enumerate(gsrc):
            dma_in("sp", gam[:, i, :], g.rearrange("(k p) -> p k", p=128), "gam", "c0",
                   allow_slow_non_contiguous=True)
        for i in range(2):
            dma_in("sp", dvec[:, i, :], ssm_d[i].rearrange("(k p) -> p k", p=128), "dvec", "c0",
                   allow_slow_non_contiguous=True)
            dma_in("sp", sink_t[:, i, :], sinks[i].partition_broadcast(128), "sink", "c0")
        S.add("dve", I("memset", ones_bf, 1.0 / D), writes=("ones",))
        S.add("dve", I("tensor_copy", ident_bf, ident), reads=("ident",), writes=("identbf",))
        S.add("dve", I("memset", xcar, 0.0), writes=("xcar",))
        S.add("dve", I("memset", eps_t, EPS), writes=("eps",))
        S.add("dve", I("memset", hpi_t, math.pi / 2), writes=("hpi",))
        G_MIX, G_MLP, G_PLE, G_KV, G_FIN = 0, 4, 8, 12, 13

        def load_w(dst, src_ap, key, chan="w"):
            S.add("pool", I("dma_start", out=dst, in_=src_ap), writes=(key,), chan=chan)

        def rmsnorm(h_t, hkey, gidx, okey, n, sq_t, rstd_t, out_of, view=None):
            vw = (lambda a: a) if view is None else view
            for k in range(KD):
                S.add("act", I("activation", sq_t[:, k, 0:n], h_t[:, k, 0:n], AF.Square),
                      reads=(hkey,), writes=(("sq", k),))
            b = next_ps()
            ps = psb[b][:, 0:n]
            S.add("pe", [I("matmul", ps, ones_bf, sq_t[:, k, 0:n], start=(k == 0), stop=(k == KD - 1))
                         for k in range(KD)],
                  reads=[("sq", k) for k in range(KD)] + ["ones"], writes=(("ps", b),))
            S.add("act", I("activation", rstd_t[:, 0:n], ps, AF.Sqrt, bias=eps_t, scale=1.0),
                  reads=(("ps", b), "eps"), writes=("rstd",))
            S.add("dve", I("reciprocal", rstd_t[:, 0:n], rstd_t[:, 0:n]), reads=("rstd",), writes=("rstd",))
            for k in range(KD):
                S.add("dve", I("scalar_tensor_tensor", out_of(k), vw(h_t[:, k, 0:n]), gam[:, gidx, k:k + 1],
                               vw(rstd_t[:, 0:n]), ALU.mult, ALU.mult),
                      reads=(hkey, "rstd", "gam"), writes=((okey, k),))

        def gemm(w_t, wkey, kin, act_of, act_keys, mlist, n, evac):
            for m in mlist:
                b = next_ps()
                ps = psb[b][:, 0:n]
                S.add("pe", [I("matmul", ps, w_t[:, k, m * 128:(m + 1) * 128], act_of(k),
                               start=(k == 0), stop=(k == kin - 1)) for k in range(kin)],
                      reads=[wkey] + list(act_keys), writes=(("ps", b),))
                evac(m, b, ps)

        def tile_loader(src_ap, bufs, keyname, chan, src_key, nslots=2):
            def load(i, t0, n, col0):
                hb = bufs[i % nslots]
                S.add("sp", I("dma_start", out=hb[:, :, 0:n],
                              in_=src_ap[:, col0:col0 + n].rearrange("(k p) n -> p k n", p=128)),
                      reads=((src_key, t0),), writes=((keyname, i % nslots),), chan=f"{chan}{i % nslots}")
            return load

        def s5_prep(li, SIN, OUT, KTt, alpha):
            base = A.off
            lr = A.alloc([NPAIR], F32); lim = A.alloc([NPAIR], F32); dtt = A.alloc([NPAIR], F32)
            t1 = A.alloc([NPAIR], F32); t2 = A.alloc([NPAIR], F32); t3 = A.alloc([NPAIR], F32)
            akr = A.alloc([T + 1, NPAIR], F32); aki = A.alloc([T + 1, NPAIR], F32)
            cfr = A.alloc([NPAIR], F32); cfi = A.alloc([NPAIR], F32)
            bR = A.alloc([NPAIR, 16], F32); bI = A.alloc([NPAIR, 16], F32)
            bbr = A.alloc([NPAIR, 16], F32); bbi = A.alloc([NPAIR, 16], F32)
            zr = A.alloc([NPAIR, 16], F32); zi = A.alloc([NPAIR, 16], F32)
            tb1 = A.alloc([NPAIR, 16], F32); tb2 = A.alloc([NPAIR, 16], F32)
            cnat = A.alloc([2, KD, 128], F32)
            cR = A.alloc([NPAIR, 16], F32); cI = A.alloc([NPAIR, 16], F32)
            zpr = A.alloc([128], F32); zpi = A.alloc([128], F32)
            cpr = A.alloc([KD, 128], F32); cpi = A.alloc([KD, 128], F32)
            PK = "prep"

            def l1(src):
                return src.rearrange("(q e) n -> e n q", e=2)
            for e_ in range(2):
                sl = slice(64 * e_, 64 * e_ + 64)
                dma_in("sp", lr[sl, :], l1(lam_re[li])[e_], PK, "c0", allow_slow_non_contiguous=True)
                dma_in("sp", lim[sl, :], l1(lam_im[li])[e_], PK, "c0", allow_slow_non_contiguous=True)
                dma_in("sp", dtt[sl, :], log_dt[li].rearrange("(q e) -> e q", e=2)[e_].partition_broadcast(64),
                       PK, "c0", allow_slow_non_contiguous=True)
                dma_in("sp", bR[sl, :, :], b_re[li].rearrange("(q e) n h -> e n q h", e=2)[e_], PK, "c0",
                       allow_slow_non_contiguous=True)
                dma_in("sp", bI[sl, :, :], b_im[li].rearrange("(q e) n h -> e n q h", e=2)[e_], PK, "c0",
                       allow_slow_non_contiguous=True)
            for r, csrc in enumerate((c_re, c_im)):
                for dup in range(2):
                    dma_in("sp", cnat[:, r, :, 64 * dup:64 * dup + 64],
                           csrc[li].rearrange("(k g) h n -> (g h) k n", k=KD), PK, "c0")

            def V(eng, inst, r=(PK,), w=(PK,)):
                S.add(eng, inst, reads=r, writes=w)

            def TT(o, a, b, op):
                V("dve", I("tensor_tensor", o, a, b, op))
            def exp_small(o, z):
                V("dve", I("tensor_scalar", o, z, 1.0 / 6, 1.0, ALU.mult, ALU.add))
                for dv in (5.0, 4.0, 3.0, 2.0, 1.0):
                    TT(o, o, z, ALU.mult)
                    V("dve", I("tensor_scalar", o, o, 1.0 / dv, 1.0, ALU.mult, ALU.add))
            V("dve", I("tensor_scalar_mul", t3, dtt, 1.0 / 64))
            exp_small(dtt, t3)
            for _ in range(6):
                TT(dtt, dtt, dtt, ALU.mult)
            TT(t3, lr, dtt, ALU.mult)
            exp_small(t1, t3)
            TT(t2, lim, dtt, ALU.mult)
            V("act", I("activation", aki[:, 1, :], t2, AF.Sin, scale=1.0 / 16), r=(PK, "hpi"))
            V("act", I("activation", akr[:, 1, :], t2, AF.Sin, bias=hpi_t, scale=1.0 / 16), r=(PK, "hpi"))
            for _ in range(4):
                TT(t3, akr[:, 1, :], aki[:, 1, :], ALU.mult)
                TT(akr[:, 1, :], akr[:, 1, :], akr[:, 1, :], ALU.mult)
                TT(aki[:, 1, :], aki[:, 1, :], aki[:, 1, :], ALU.mult)
                TT(akr[:, 1, :], akr[:, 1, :], aki[:, 1, :], ALU.subtract)
                V("dve", I("tensor_scalar_mul", aki[:, 1, :], t3, 2.0))
            TT(t3, akr[:, 1, :], akr[:, 1, :], ALU.mult)
            TT(t2, aki[:, 1, :], aki[:, 1, :], ALU.mult)
            TT(t3, t3, t2, ALU.add)
            V("dve", I("tensor_scalar", t3, t3, -0.5, 1.5, ALU.mult, ALU.add))
            TT(t1, t1, t3, ALU.mult)
            TT(akr[:, 1, :], akr[:, 1, :], t1, ALU.mult)
            TT(aki[:, 1, :], aki[:, 1, :], t1, ALU.mult)
            V("dve", I("memset", akr[:, 0, :], 1.0))
            V("dve", I("memset", aki[:, 0, :], 0.0))
            for k in range(2, T + 1):
                TT(t1, akr[:, k - 1, :], akr[:, 1, :], ALU.mult)
                TT(t2, aki[:, k - 1, :], aki[:, 1, :], ALU.mult)
                TT(akr[:, k, :], t1, t2, ALU.subtract)
                TT(t1, akr[:, k - 1, :], aki[:, 1, :], ALU.mult)
                TT(t2, aki[:, k - 1, :], akr[:, 1, :], ALU.mult)
                TT(aki[:, k, :], t1, t2, ALU.add)
            dump("dtt", dtt, [PK]); dump("akr", akr, [PK]); dump("aki", aki, [PK])
            V("dve", I("tensor_copy", alpha[:, 0, :], akr[:, T, :]), w=(PK, "alpha"))
            V("dve", I("tensor_copy", alpha[:, 1, :], aki[:, T, :]), w=(PK, "alpha"))
            V("dve", I("tensor_scalar_add", t1, akr[:, 1, :], -1.0))
            TT(t2, lr, lr, ALU.mult)
            TT(t3, lim, lim, ALU.mult)
            TT(t2, t2, t3, ALU.add)
            V("dve", I("reciprocal", t2, t2))
            TT(cfr, t1, lr, ALU.mult)
            TT(t3, aki[:, 1, :], lim, ALU.mult)
            TT(cfr, cfr, t3, ALU.add)
            TT(cfr, cfr, t2, ALU.mult)
            TT(cfi, aki[:, 1, :], lr, ALU.mult)
            TT(t3, t1, lim, ALU.mult)
            TT(cfi, cfi, t3, ALU.subtract)
            TT(cfi, cfi, t2, ALU.mult)

            def bc(v):
                return v.unsqueeze(2).to_broadcast([128, NPAIR, 16])

            def cmul(outr, outi, sr, si, vr, vi):
                TT(tb1, vr, bc(sr), ALU.mult)
                TT(tb2, vi, bc(si), ALU.mult)
                TT(outr, tb1, tb2, ALU.subtract)
                TT(tb1, vi, bc(sr), ALU.mult)
                TT(tb2, vr, bc(si), ALU.mult)
                TT(outi, tb1, tb2, ALU.add)
            cmul(bbr, bbi, cfr, cfi, bR, bI)
            dump("cfr", cfr, [PK]); dump("cfi", cfi, [PK]); dump("bbr", bbr, [PK]); dump("bbi", bbi, [PK])

            for r, dstc in enumerate((cR, cI)):
                for k in range(KD):
                    b = next_ps()
                    ps = psb[b][:, 0:128]
                    S.add("pe", I("transpose", ps, cnat[:, r, k, :], ident),
                          reads=(PK, "ident"), writes=(("ps", b),))
                    psv = ps.rearrange("p (q e h) -> p q e h", q=4, e=2)
                    S.add("act", I("activation", dstc[0:64, 4 * k:4 * k + 4, :], psv[0:64, :, 0, :], AF.Copy),
                          reads=(("ps", b),), writes=(PK,))
                    S.add("act", I("activation", dstc[64:128, 4 * k:4 * k + 4, :], psv[64:128, :, 1, :], AF.Copy),
                          reads=(("ps", b),), writes=(PK,))
            dump("cR", cR, [PK]); dump("cI", cI, [PK])
            S.add("pool", I("memset", OUT, 0.0), writes=("OUT",))
            for j in range(T):
                cmul(zr, zi, akr[:, j + 1, :], aki[:, j + 1, :], cR, cI)
                for e_ in range(2):
                    sl = slice(64 * e_, 64 * e_ + 64)
                    S.add("act", I("activation", OUT[sl, :, j, 0, 16 * e_:16 * e_ + 16], zr[sl, :, :], AF.Copy),
                          reads=(PK,), writes=("OUT",))
                    S.add("act", I("activation", OUT[sl, :, j, 1, 16 * e_:16 * e_ + 16], zi[sl, :, :], AF.Copy,
                                   scale=-1.0), reads=(PK,), writes=("OUT",))
            V("dve", I("memset", cpr, 0.0))
            V("dve", I("memset", cpi, 0.0))
            for k in range(KD):
                for e_ in range(2):
                    sl = slice(64 * e_, 64 * e_ + 64)
                    dv_r = cpr[:, k, :].rearrange("p (q e h) -> p q e h", q=4, e=2)
                    dv_i = cpi[:, k, :].rearrange("p (q e h) -> p q e h", q=4, e=2)
                    V("dve", I("tensor_copy", dv_r[sl, :, e_, :], cR[sl, 4 * k:4 * k + 4, :]))
                    V("dve", I("tensor_scalar_mul", dv_i[sl, :, e_, :], cI[sl, 4 * k:4 * k + 4, :], -1.0))
            V("dve", I("memset", zpr, 0.0))
            V("dve", I("memset", zpi, 0.0))
            zvr = zpr.rearrange("p (q e h) -> p q e h", q=4, e=2)
            zvi = zpi.rearrange("p (q e h) -> p q e h", q=4, e=2)
            for kk in range(T):
                cmul(zr, zi, akr[:, kk, :], aki[:, kk, :], bbr, bbi)
                s_ = T - 1 - kk
                for k in range(KD):
                    for e_ in range(2):
                        sl = slice(64 * e_, 64 * e_ + 64)
                        V("dve", I("tensor_copy", zvr[sl, :, e_, :], zr[sl, 4 * k:4 * k + 4, :]))
                        V("dve", I("tensor_copy", zvi[sl, :, e_, :], zi[sl, 4 * k:4 * k + 4, :]))
                    for part, zp in enumerate((zpr, zpi)):
                        b = next_ps()
                        ps = psb[b][:, 0:128]
                        S.add("pe", I("transpose", ps, zp, ident), reads=(PK, "ident"), writes=(("ps", b),))
                        S.add("act", I("activation", SIN[:, k, s_, part, :], ps, AF.Copy),
                              reads=(("ps", b),), writes=("SIN",))
                    b = next_ps()
                    ps = psb[b][:, 0:128]
                    S.add("pe", [I("matmul", ps, zpr, cpr[:, k, :], start=True, stop=False),
                                 I("matmul", ps, zpi, cpi[:, k, :], start=False, stop=True)],
                          reads=(PK,), writes=(("ps", b),))
                    S.add("dve", I("tensor_tensor", KTt[:, k, kk, :], ps, bdmask, ALU.mult),
                          reads=(("ps", b), "bdmask"), writes=("KT", PK))
            A.reset(base)

        def s5_phase(li, src, dst, out_start):
            A.reset(PERSIST)
            ps_pool[0] = list(range(8))
            SIN = A.alloc([KD, T, 2, 128], BF16)
            OUT = A.alloc([NPAIR, T, 2, 32], BF16)
            KTt = A.alloc([KD, T, 128], BF16)
            alpha = A.alloc([2, NPAIR], F32)
            wg = A.alloc([KD, 2 * D], BF16)
            load_w(wg, w_glu[li].rearrange("(k p) o -> p k o", p=128), "wglu")
            s5_prep(li, SIN, OUT, KTt, alpha)
            S.barrier()
            NMAX = 256
            CM = NMAX // T
            hbuf = [A.alloc([KD, NMAX], F32) for _ in range(3)]
            sq_t = A.alloc([KD, NMAX], BF16)
            rstd_t = A.alloc([NMAX], F32)
            u_bufs = [A.alloc([KD, NMAX], BF16) for _ in range(2)]
            S_bufs = [A.alloc([2, NPAIR, CM], F32) for _ in range(2)]
            X = A.alloc([2, NPAIR, CM + 1], F32)
            X_bf = A.alloc([2, NPAIR, CM], BF16)
            tA = A.alloc([NPAIR], F32); tB = A.alloc([NPAIR], F32)
            tC = A.alloc([NPAIR], F32); tD = A.alloc([NPAIR], F32)
            v_l = [A.alloc([NMAX], F32) for _ in range(2)]; w_l = [A.alloc([NMAX], F32) for _ in range(2)]
            sg_l = [A.alloc([NMAX], F32) for _ in range(2)]
            g_bf = A.alloc([KD, NMAX], BF16)
            S.add("dve", I("memset", xcar, 0.0), writes=("xcar",))
            rem = (WIN - out_start) % NMAX
            tl = tiles_of(0, out_start, NMAX) + tiles_of(out_start, rem, NMAX) + \
                tiles_of(out_start + rem, WIN - out_start - rem, NMAX)
            load = tile_loader(src, hbuf, "h", "h", "src", nslots=3)
            SK = ("S", "X", "xcar", "alpha")
            ar, ai = alpha[:, 0, :], alpha[:, 1, :]

            def stage1(i, t0, n):
                hb = hbuf[i % 3]
                hkey = ("h", i % 3)
                u_bf = u_bufs[i % 2]
                Ssb = S_bufs[i % 2]
                UK = "u%d" % (i % 2)
                SKEY = "S%d" % (i % 2)
                C = n // T
                pv = lambda a: a.rearrange("p (c s) -> p c s", s=T)
                rmsnorm(hb, hkey, G_MIX + li, UK, n, sq_t, rstd_t,
                        out_of=lambda k: u_bf[:, k, 0:n].rearrange("p (s c) -> p c s", s=T), view=pv)
                for k in range(KD):
                    for part in range(2):
                        banks = [next_ps() for _ in range(4)]
                        ops = []
                        for s_ in range(T):
                            for qq in range(4):
                                r0 = 32 * qq
                                ops.append(I("matmul", psb[banks[qq]][:, 0:C], SIN[r0:r0 + 32, k, s_, part, :],
                                             u_bf[r0:r0 + 32, k, s_ * C:(s_ + 1) * C],
                                             start=(s_ == 0), stop=(s_ == T - 1), tile_position=(r0, 0)))
                        S.add("pe", ops, reads=("SIN", (UK, k)), writes=tuple(("ps", b_) for b_ in banks))
                        for qq in range(4):
                            S.add("act", I("activation", Ssb[:, part, 4 * k + qq, 0:C], psb[banks[qq]][:, 0:C], AF.Copy),
                                  reads=(("ps", banks[qq]),), writes=(SKEY,))

            def stage2(i, t0, n):
                hb = hbuf[i % 3]
                hkey = ("h", i % 3)
                u_bf = u_bufs[i % 2]
                Ssb = S_bufs[i % 2]
                UK = "u%d" % (i % 2)
                SKEY = "S%d" % (i % 2)
                SK = (SKEY, "X", "xcar", "alpha")
                C = n // T
                full = t0 >= out_start
                S.add("dve", I("tensor_copy", X[:, :, :, 0], xcar), reads=SK, writes=("X",))
                for c in range(C):
                    xr, xi = X[:, 0, :, c], X[:, 1, :, c]
                    S.add("dve", I("tensor_tensor", tA, xr, ar, ALU.mult), reads=("X", "alpha", "xcar"), writes=("tA",))
                    S.add("dve", I("tensor_tensor", tB, xi, ai, ALU.mult), reads=("X", "alpha"), writes=("tB",))
                    S.add("dve", I("tensor_tensor", tC, xi, ar, ALU.mult), reads=("X", "alpha"), writes=("tC",))
                    S.add("dve", I("tensor_tensor", tD, xr, ai, ALU.mult), reads=("X", "alpha"), writes=("tD",))
                    S.add("dve", I("tensor_tensor", tA, tA, tB, ALU.subtract), reads=("tA", "tB"), writes=("tA",))
                    S.add("dve", I("tensor_tensor", tC, tC, tD, ALU.add), reads=("tC", "tD"), writes=("tC",))
                    S.add("dve", I("tensor_tensor", X[:, 1, :, c + 1], tC, Ssb[:, 1, :, c], ALU.add),
                          reads=("tC", SKEY), writes=("X",))
                    S.add("dve", I("tensor_tensor", X[:, 0, :, c + 1], tA, Ssb[:, 0, :, c], ALU.add),
                          reads=("tA", SKEY), writes=("X",))
                S.add("dve", I("tensor_copy", xcar, X[:, :, :, C]), reads=SK, writes=("xcar",))
                if not full:
                    return
                S.add("act", I("activation", X_bf[:, :, :, 0:C], X[:, :, :, 0:C], AF.Copy),
                      reads=("X",), writes=("Xbf",))
                for k in range(KD):
                    v_t, w_t2, sg_t = v_l[k % 2], w_l[k % 2], sg_l[k % 2]
                    VK, WK, GK = "v%d" % (k % 2), "w%d" % (k % 2), "sg%d" % (k % 2)
                    b = next_ps()
                    ps = psb[b][:, 0:n]
                    ops = []
                    for d_ in range(T):
                        ops.append(I("matmul", ps[:, d_ * C:T * C], KTt[:, k, d_, :], u_bf[:, k, 0:(T - d_) * C],
                                     start=(d_ == 0), stop=False))
                    for j in range(T):
                        for part in range(2):
                            for qq in range(4):
                                q = 4 * k + qq
                                ops.append(I("matmul", ps[32 * qq:32 * qq + 32, j * C:(j + 1) * C],
                                             OUT[:, q, j, part, :], X_bf[:, part, q, 0:C],
                                             start=False, stop=(j == T - 1 and part == 1),
                                             tile_position=(0, 32 * qq)))
                    S.add("pe", ops, reads=("KT", "OUT", "Xbf", (UK, k)), writes=(("ps", b),))
                    S.add("dve", I("scalar_tensor_tensor", v_t[:, 0:n], u_bf[:, k, 0:n], dvec[:, li, k:k + 1], ps,
                                   ALU.mult, ALU.add), reads=(("ps", b), (UK, k), "dvec"), writes=(VK,))
                    S.add("act", I("activation", w_t2[:, 0:n], v_t[:, 0:n], AF.Square), reads=(VK,), writes=(WK,))
                    S.add("dve", I("tensor_scalar", w_t2[:, 0:n], w_t2[:, 0:n], 0.044715, 1.0, ALU.mult, ALU.add),
                          reads=(WK,), writes=(WK,))
                    S.add("dve", I("tensor_tensor", w_t2[:, 0:n], w_t2[:, 0:n], v_t[:, 0:n], ALU.mult),
                          reads=(WK, VK), writes=(WK,))
                    S.add("act", I("activation", sg_t[:, 0:n], w_t2[:, 0:n], AF.Sigmoid, scale=1.5957691216057308),
                          reads=(WK,), writes=(GK,))
                    S.add("dve", I("tensor_tensor", g_bf[:, k, 0:n], v_t[:, 0:n], sg_t[:, 0:n], ALU.mult),
                          reads=(VK, GK), writes=(("g", k),))
                gkeys = [("g", k) for k in range(KD)]
                for m in range(KD):
                    v_t, sg_t = v_l[m % 2], sg_l[m % 2]
                    VK, GK = "v%d" % (m % 2), "sg%d" % (m % 2)
                    ba = next_ps(); bb_ = next_ps()
                    psa = psb[ba][:, 0:n]; psg = psb[bb_][:, 0:n]
                    S.add("pe", [I("matmul", psa, wg[:, k, m * 128:(m + 1) * 128], g_bf[:, k, 0:n],
                                   start=(k == 0), stop=(k == KD - 1)) for k in range(KD)] +
                                [I("matmul", psg, wg[:, k, D + m * 128:D + (m + 1) * 128], g_bf[:, k, 0:n],
                                   start=(k == 0), stop=(k == KD - 1)) for k in range(KD)],
                          reads=["wglu"] + gkeys, writes=(("ps", ba), ("ps", bb_)))
                    S.add("act", I("activation", sg_t[:, 0:n], psg, AF.Sigmoid), reads=(("ps", bb_),), writes=(GK,))
                    S.add("dve", I("tensor_tensor", v_t[:, 0:n], psa, sg_t[:, 0:n], ALU.mult),
                          reads=(("ps", ba), GK), writes=(VK,))
                    hv = hb[:, m, 0:n].rearrange("p (c s) -> p s c", s=T)
                    S.add("dve", I("tensor_tensor", hv, hv, v_t[:, 0:n].rearrange("p (s c) -> p s c", s=T), ALU.add),
                          reads=(VK, hkey), writes=(hkey,))
                c0 = t0 - out_start
                S.add("sp", I("dma_start", out=dst[:, c0:c0 + n].rearrange("(k p) n -> p k n", p=128),
                              in_=hb[:, :, 0:n]), reads=(hkey,), writes=(("dst", t0),), chan=f"st{i % 3}")

            for i0 in range(min(2, len(tl))):
                load(i0, tl[i0][0], tl[i0][1], tl[i0][0])
            stage1(0, tl[0][0], tl[0][1])
            for i, (t0, n) in enumerate(tl):
                if i + 2 < len(tl):
                    load(i + 2, tl[i + 2][0], tl[i + 2][1], tl[i + 2][0])
                if i + 1 < len(tl):
                    stage1(i + 1, tl[i + 1][0], tl[i + 1][1])
                stage2(i, t0, n)
            S.barrier()

        def mlp_phase(layer, buf, start, total):
            A.reset(PERSIST)
            ps_pool[0] = list(range(8))
            wu = A.alloc([KD, DFF], BF16)
            wd = A.alloc([DFF // 128, D], BF16)
            for k in range(KD):
                load_w(wu[:, k, :], w_up[layer][k * 128:(k + 1) * 128, :], "wu")
            for k4 in range(0, DFF // 128, 8):
                load_w(wd[:, k4:k4 + 8, :],
                       w_down[layer][k4 * 128:(k4 + 8) * 128, :].rearrange("(k p) o -> p k o", p=128), "wd")
            NM = 256
            hbuf = [A.alloc([KD, NM], F32) for _ in range(2)]
            sq_t = A.alloc([KD, NM], BF16)
            rstd_t = A.alloc([NM], F32)
            hm = A.alloc([KD, NM], BF16)
            ff = A.alloc([DFF // 128, NM], BF16)
            r_t = [A.alloc([NM], F32) for _ in range(2)]
            tl = tiles_of(start, total % NM, NM) + tiles_of(start + total % NM, total - total % NM, NM)
            load = tile_loader(buf, hbuf, "h", "h", "buf")

            def do_tile(i, t0, n):
                hb = hbuf[i % 2]
                hkey = ("h", i % 2)
                rmsnorm(hb, hkey, G_MLP + layer, "hm", n, sq_t, rstd_t, out_of=lambda k: hm[:, k, 0:n])
                hmk = [("hm", k) for k in range(KD)]

                def ev_up(m, b, ps):
                    rt = r_t[m % 2]
                    S.add("act", I("activation", rt[:, 0:n], ps, AF.Relu), reads=(("ps", b),), writes=(("r", m % 2),))
                    S.add("dve", I("tensor_tensor", ff[:, m, 0:n], rt[:, 0:n], rt[:, 0:n], ALU.mult),
                          reads=(("r", m % 2),), writes=(("ff", m),))
                gemm(wu, "wu", KD, lambda k: hm[:, k, 0:n], hmk, range(DFF // 128), n, ev_up)
                ffk = [("ff", m) for m in range(DFF // 128)]

                def ev_dn(m, b, ps):
                    S.add("dve", I("tensor_tensor", hb[:, m, 0:n], hb[:, m, 0:n], ps, ALU.add),
                          reads=(("ps", b), hkey), writes=(hkey,))
                gemm(wd, "wd", DFF // 128, lambda k: ff[:, k, 0:n], ffk, range(KD), n, ev_dn)
                S.add("sp", I("dma_start", out=buf[:, t0:t0 + n].rearrange("(k p) n -> p k n", p=128),
                              in_=hb[:, :, 0:n]), reads=(hkey,), writes=(("buf", t0),), chan=f"st{i % 2}")

            load(0, tl[0][0], tl[0][1], tl[0][0])
            for i, (t0, n) in enumerate(tl):
                if i + 1 < len(tl):
                    load(i + 1, tl[i + 1][0], tl[i + 1][1], tl[i + 1][0])
                do_tile(i, t0, n)
            S.barrier()

        def ple_phase(layer, buf, start, total, p_src, do_kv=False, final=False):
            A.reset(PERSIST)
            ps_pool[0] = list(range(8))
            wgt = A.alloc([KD, D], BF16)
            wpj = A.alloc([2, D], BF16)
            load_w(wgt, w_gate[layer].rearrange("(k p) o -> p k o", p=128), "wgt")
            load_w(wpj, w_proj[layer].rearrange("(k p) o -> p k o", p=128), "wpj")
            NM = 512
            if do_kv:
                wk2 = A.alloc([KD, NKV, 128], BF16)
                wks2 = A.alloc([KD, NKV, 128], BF16)
                wv = A.alloc([KD, 256], BF16)
                for dup in range(2):
                    for k in range(KD):
                        load_w(wk2[:, k, :, 64 * dup:64 * dup + 64],
                               w_k[k * 128:(k + 1) * 128, :].rearrange("p (v d) -> p v d", d=64), "wk2")
                        load_w(wks2[:, k, :, 64 * dup:64 * dup + 64],
                               w_ks[k * 128:(k + 1) * 128, :].rearrange("p (v d) -> p v d", d=64), "wks2")
                load_w(wv, w_v.rearrange("(k p) o -> p k o", p=128), "wv")
                cs_t = A.alloc([2, NM], F32)
                kt_t = A.alloc([NKV, NM], BF16)
                vt_t = A.alloc([NM // 128, 256], BF16)
            hbuf = [A.alloc([KD, NM], F32) for _ in range(2)]
            pbuf = [A.alloc([2, NM], F32) for _ in range(2)]
            p_bf = A.alloc([2, NM], BF16)
            sq_t = A.alloc([KD, NM], BF16)
            rstd_t = A.alloc([NM], F32)
            hp = A.alloc([KD, NM], BF16)
            gate_t = A.alloc([NM], F32)
            tmp_t = A.alloc([NM], F32)
            tl = tiles_of(start, total % NM, NM) + tiles_of(start + total % NM, total - total % NM, NM)
            loadh = tile_loader(buf, hbuf, "h", "h", "buf")
            loadp = tile_loader(p_src, pbuf, "p", "p", "psrc")

            def load(i):
                t0, n = tl[i]
                loadh(i, t0, n, t0)
                loadp(i, t0, n, t0 - start)

            def do_tile(i, t0, n):
                hb = hbuf[i % 2]
                hkey = ("h", i % 2)
                pb = pbuf[i % 2]
                S.add("act", I("activation", p_bf[:, :, 0:n], pb[:, :, 0:n], AF.Copy),
                      reads=(("p", i % 2),), writes=("pbf",))
                rmsnorm(hb, hkey, G_PLE + layer, "hp", n, sq_t, rstd_t, out_of=lambda k: hp[:, k, 0:n])
                hpk = [("hp", k) for k in range(KD)]
                for m in range(KD):
                    bg = next_ps(); bp = next_ps()
                    psg = psb[bg][:, 0:n]; psp = psb[bp][:, 0:n]
                    S.add("pe", [I("matmul", psg, wgt[:, k, m * 128:(m + 1) * 128], hp[:, k, 0:n],
                                   start=(k == 0), stop=(k == KD - 1)) for k in range(KD)] +
                                [I("matmul", psp, wpj[:, 0, m * 128:(m + 1) * 128], p_bf[:, 0, 0:n], start=True, stop=False),
                                 I("matmul", psp, wpj[:, 1, m * 128:(m + 1) * 128], p_bf[:, 1, 0:n], start=False, stop=True)],
                          reads=["wgt", "wpj", "pbf"] + hpk, writes=(("ps", bg), ("ps", bp)))
                    S.add("act", I("activation", gate_t[:, 0:n], psg, AF.Sigmoid), reads=(("ps", bg),), writes=("gate",))
                    S.add("dve", I("tensor_tensor", tmp_t[:, 0:n], psp, gate_t[:, 0:n], ALU.mult),
                          reads=(("ps", bp), "gate"), writes=("tmp",))
                    S.add("dve", I("tensor_tensor", hb[:, m, 0:n], hb[:, m, 0:n], tmp_t[:, 0:n], ALU.add),
                          reads=("tmp", hkey), writes=(hkey,))
                if final:
                    rmsnorm(hb, hkey, G_FIN, "fin", n, sq_t, rstd_t, out_of=lambda k: hb[:, k, 0:n])
                    S.add("sp", I("dma_start", out=outT[:, t0 - start:t0 - start + n].rearrange("(k p) n -> p k n", p=128),
                                  in_=hb[:, :, 0:n]), reads=[hkey] + [("fin", k) for k in range(KD)],
                          writes=(("out", t0),), chan=f"st{i % 2}")
                else:
                    S.add("sp", I("dma_start", out=buf[:, t0:t0 + n].rearrange("(k p) n -> p k n", p=128),
                                  in_=hb[:, :, 0:n]), reads=(hkey,), writes=(("buf", t0),), chan=f"st{i % 2}")
                if do_kv:
                    a0 = t0 - start
                    S.add("sp", I("dma_start", out=cs_t[:, 0, 0:n], in_=cosT[:, a0:a0 + n]), writes=("cs",), chan="cs")
                    S.add("sp", I("dma_start", out=cs_t[:, 1, 0:n], in_=sinT[:, a0:a0 + n]), writes=("cs",), chan="cs")
                    rmsnorm(hb, hkey, G_KV, "hk", n, sq_t, rstd_t, out_of=lambda k: hp[:, k, 0:n])
                    hkk = [("hk", k) for k in range(KD)]
                    for kv in range(NKV):
                        b1 = next_ps(); b2 = next_ps()
                        ps1 = psb[b1][:, 0:n]; ps2 = psb[b2][:, 0:n]
                        S.add("pe", [I("matmul", ps1, wk2[:, k, kv, :], hp[:, k, 0:n], start=(k == 0), stop=(k == KD - 1))
                                     for k in range(KD)] +
                                    [I("matmul", ps2, wks2[:, k, kv, :], hp[:, k, 0:n], start=(k == 0), stop=(k == KD - 1))
                                     for k in range(KD)],
                              reads=["wk2", "wks2"] + hkk, writes=(("ps", b1), ("ps", b2)))
                        S.add("dve", I("tensor_tensor", gate_t[:, 0:n], ps1, cs_t[:, 0, 0:n], ALU.mult),
                              reads=(("ps", b1), "cs"), writes=("gate",))
                        S.add("dve", I("tensor_tensor", tmp_t[:, 0:n], ps2, cs_t[:, 1, 0:n], ALU.mult),
                              reads=(("ps", b2), "cs"), writes=("tmp",))
                        S.add("dve", I("tensor_tensor", kt_t[:, kv, 0:n], gate_t[:, 0:n], tmp_t[:, 0:n], ALU.add),
                              reads=("gate", "tmp"), writes=("ktt",))
                    S.add("sp", I("dma_start", out=kT2_d[:, :, a0:a0 + n], in_=kt_t[:, :, 0:n]),
                          reads=("ktt",), writes=("kT2d",), chan="kst")
                    for blk in range(n // 128):
                        b = next_ps()
                        ps = psb[b][:, 0:256]
                        S.add("pe", [I("matmul", ps, hp[:, k, blk * 128:(blk + 1) * 128], wv[:, k, :],
                                       start=(k == 0), stop=(k == KD - 1)) for k in range(KD)],
                              reads=["wv"] + hkk, writes=(("ps", b),))
                        S.add("act", I("activation", vt_t[:, blk, :], ps, AF.Copy), reads=(("ps", b),), writes=("vtt",))
                    S.add("sp", I("dma_start", out=v_d[:, a0 // 128:a0 // 128 + n // 128, :], in_=vt_t[:, 0:n // 128, :]),
                          reads=("vtt",), writes=("vd",), chan="vst")

            load(0)
            for i, (t0, n) in enumerate(tl):
                if i + 1 < len(tl):
                    load(i + 1)
                do_tile(i, t0, n)
            S.barrier()

        def attn_phase(layer):
            j = layer - 2
            A.reset(PERSIST)
            ps_pool[0] = list(range(6))
            wq = A.alloc([KD, D], BF16)
            wqs = A.alloc([KD, D], BF16)
            wo = A.alloc([KD, D], BF16)
            load_w(wq, w_q[j].rearrange("(k p) o -> p k o", p=128), "wq")
            load_w(wqs, w_qs[j].rearrange("(k p) o -> p k o", p=128), "wqs")
            load_w(wo, w_o[j].rearrange("(k p) o -> p k o", p=128), "wo")
            kT2 = A.alloc([NKV, OWNH], BF16)
            v_bf = A.alloc([NBLK, 256], BF16)
            S.add("sp", I("dma_start", out=kT2, in_=kT2_d), writes=("kT2",), chan="kld")
            S.add("sp", I("dma_start", out=v_bf, in_=v_d), writes=("vbf",), chan="kld")
            NM = 512
            hbuf = [A.alloc([KD, NM], F32) for _ in range(2)]
            cs_t = A.alloc([2, NM], F32)
            sq_t = A.alloc([KD, NM], BF16)
            rstd_t = A.alloc([NM], F32)
            hn = A.alloc([KD, NM], BF16)
            qT = A.alloc([KD, NM], BF16)
            t1 = A.alloc([NM], F32); t2 = A.alloc([NM], F32)
            sm_l = [A.alloc([2, 256], F32) for _ in range(2)]
            e_l = [A.alloc([2, 256], BF16) for _ in range(2)]
            eT_l = [A.alloc([2, 2, 128], BF16) for _ in range(2)]
            stat_l = [A.alloc([8], F32) for _ in range(2)]
            rinv = A.alloc([NH], F32)
            o_bf = A.alloc([D], BF16)
            oT = A.alloc([KD, NM], BF16)
            tl = tiles_of(HALO, OWN, NM)
            load = tile_loader(hB, hbuf, "h", "h", "buf")

            def do_tile(i, t0, n):
                hb = hbuf[i % 2]
                hkey = ("h", i % 2)
                S.add("sp", I("dma_start", out=cs_t[:, 0, 0:n], in_=cosT[:, t0:t0 + n]), writes=("cs",), chan="cs")
                S.add("sp", I("dma_start", out=cs_t[:, 1, 0:n], in_=sinT[:, t0:t0 + n]), writes=("cs",), chan="cs")
                rmsnorm(hb, hkey, G_MIX + layer, "hn", n, sq_t, rstd_t, out_of=lambda k: hn[:, k, 0:n])
                hnk = [("hn", k) for k in range(KD)]
                for m in range(KD):
                    b1 = next_ps(); b2 = next_ps()
                    ps1 = psb[b1][:, 0:n]; ps2 = psb[b2][:, 0:n]
                    S.add("pe", [I("matmul", ps1, wq[:, k, m * 128:(m + 1) * 128], hn[:, k, 0:n],
                                   start=(k == 0), stop=(k == KD - 1)) for k in range(KD)] +
                                [I("matmul", ps2, wqs[:, k, m * 128:(m + 1) * 128], hn[:, k, 0:n],
                                   start=(k == 0), stop=(k == KD - 1)) for k in range(KD)],
                          reads=["wq", "wqs"] + hnk, writes=(("ps", b1), ("ps", b2)))
                    S.add("dve", I("tensor_tensor", t1[:, 0:n], ps1, cs_t[:, 0, 0:n], ALU.mult),
                          reads=(("ps", b1), "cs"), writes=("t1",))
                    S.add("dve", I("tensor_tensor", t2[:, 0:n], ps2, cs_t[:, 1, 0:n], ALU.mult),
                          reads=(("ps", b2), "cs"), writes=("t2",))
                    S.add("dve", I("tensor_tensor", qT[:, m, 0:n], t1[:, 0:n], t2[:, 0:n], ALU.add),
                          reads=("t1", "t2"), writes=(("q", m),))
                for blk in range(n // 128):
                    ab = (t0 - HALO) // 128 + blk
                    msk = mask0_t if ab == 0 else maskG_t
                    mkey = "mask0" if ab == 0 else "maskG"
                    bo = [6, 7]

                    def qk_softmax(hp_, blk=blk, ab=ab, msk=msk, mkey=mkey):
                        par = hp_ % 2
                        sm, e_bf, stat = sm_l[par], e_l[par], stat_l[par]
                        bsl = [next_ps(), next_ps()]
                        for a in range(2):
                            S.add("pe", I("matmul", psb[bsl[a]][:, 0:256],
                                          qT[64 * a:64 * a + 64, hp_, blk * 128:(blk + 1) * 128],
                                          kT2[64 * a:64 * a + 64, (2 * hp_ + a) // 4, ab * 128:ab * 128 + 256],
                                          start=True, stop=True, tile_position=(64 * a, 0)),
                                  reads=(("q", hp_), "kT2"), writes=(("ps", bsl[a]),))
                        for a in range(2):
                            hd = 2 * hp_ + a
                            S.add("dve", I("scalar_tensor_tensor", sm[:, a, :], psb[bsl[a]][:, 0:256], 0.125, msk,
                                           ALU.mult, ALU.add),
                                  reads=(("ps", bsl[a]), mkey), writes=(("sm", par, a),))
                            S.add("dve", I("reduce_max", stat[:, a:a + 1], sm[:, a, :], axis=AX.X),
                                  reads=(("sm", par, a),), writes=(("mx", par, a),))
                            S.add("dve", I("tensor_scalar", stat[:, 2 + a:3 + a], stat[:, a:a + 1],
                                           sink_t[:, j, hd:hd + 1], -1.0, ALU.max, ALU.mult),
                                  reads=(("mx", par, a), "sink"), writes=(("ngm", par, a),))
                            S.add("dve", I("memset", stat[:, 4 + a:5 + a], 0.0), writes=(("rs", par, a),))
                            S.add("act", I("activation", e_bf[:, a, :], sm[:, a, :], AF.Exp, bias=stat[:, 2 + a:3 + a],
                                           scale=1.0, accum_out=stat[:, 4 + a:5 + a]),
                                  reads=(("sm", par, a), ("ngm", par, a)), writes=(("e", par, a), ("rs", par, a)))
                            S.add("act", I("activation", stat[:, 6 + a:7 + a], stat[:, 2 + a:3 + a], AF.Exp,
                                           bias=sink_t[:, j, hd:hd + 1], scale=1.0),
                                  reads=(("ngm", par, a), "sink"), writes=(("es", par, a),))
                            S.add("dve", I("tensor_tensor", stat[:, 4 + a:5 + a], stat[:, 4 + a:5 + a],
                                           stat[:, 6 + a:7 + a], ALU.add),
                                  reads=(("rs", par, a), ("es", par, a)), writes=(("rs", par, a),))
                            S.add("dve", I("reciprocal", rinv[:, hd:hd + 1], stat[:, 4 + a:5 + a]),
                                  reads=(("rs", par, a),), writes=(("rinv", hd),))

                    def tr_pv(hp_, ab=ab):
                        par = hp_ % 2
                        e_bf, eT = e_l[par], eT_l[par]
                        bt = next_ps()
                        pst = psb[bt][:].bitcast(BF16)[:, 0:512].rearrange("p (a c q) -> p a c q", a=2, c=2)
                        S.add("pe", [I("transpose", pst[:, a, c, :], e_bf[:, a, c * 128:(c + 1) * 128], ident_bf)
                                     for a in range(2) for c in range(2)],
                              reads=(("e", par, 0), ("e", par, 1), "identbf"), writes=(("ps", bt),))
                        S.add("act", I("activation", eT, pst, AF.Copy), reads=(("ps", bt),), writes=(("eT", par),))
                        pso = psb[bo[hp_ // 4]][:, 0:512].rearrange("p (h d) -> p h d", d=64)
                        S.add("pe", [I("matmul", pso[:, (2 * hp_ + a) % 8, :], eT[:, a, c, :],
                                       v_bf[:, ab + c, ((2 * hp_ + a) // 4) * 64:((2 * hp_ + a) // 4 + 1) * 64],
                                       start=(c == 0), stop=(c == 1)) for a in range(2) for c in range(2)],
                              reads=(("eT", par), "vbf"), writes=(("pso", hp_),))

                    qk_softmax(0)
                    for hp_ in range(NH // 2):
                        if hp_ + 1 < NH // 2:
                            qk_softmax(hp_ + 1)
                        tr_pv(hp_)
                    for half in range(2):
                        pso = psb[bo[half]][:, 0:512].rearrange("p (h d) -> p h d", d=64)
                        ov = o_bf[:, half * 512:(half + 1) * 512].rearrange("p (h d) -> p h d", d=64)
                        S.add("dve", I("tensor_tensor", ov, pso,
                                       rinv[:, half * 8:half * 8 + 8].unsqueeze(2).to_broadcast([128, 8, 64]), ALU.mult),
                              reads=[("pso", hp_) for hp_ in range(4 * half, 4 * half + 4)] +
                                    [("rinv", hd) for hd in range(8 * half, 8 * half + 8)],
                              writes=(("ob", half),) + tuple(("pso", hp_) for hp_ in range(4 * half, 4 * half + 4)))
                    for kk in range(KD):
                        bt = next_ps()
                        pst = psb[bt][:].bitcast(BF16)[:, 0:128]
                        S.add("pe", I("transpose", pst, o_bf[:, kk * 128:(kk + 1) * 128], ident_bf),
                              reads=(("ob", kk // 4), "identbf"), writes=(("ps", bt),))
                        S.add("act", I("activation", oT[:, kk, blk * 128:(blk + 1) * 128], pst, AF.Copy),
                              reads=(("ps", bt),), writes=(("oT", kk),))
                oTk = [("oT", kk) for kk in range(KD)]

                def ev_o(m, b, ps):
                    S.add("dve", I("tensor_tensor", hb[:, m, 0:n], hb[:, m, 0:n], ps, ALU.add),
                          reads=(("ps", b), hkey), writes=(hkey,))
                gemm(wo, "wo", KD, lambda k: oT[:, k, 0:n], oTk, range(KD), n, ev_o)
                S.add("sp", I("dma_start", out=hB[:, t0:t0 + n].rearrange("(k p) n -> p k n", p=128),
                              in_=hb[:, :, 0:n]), reads=(hkey,), writes=(("buf", t0),), chan=f"st{i % 2}")

            load(0, tl[0][0], tl[0][1], tl[0][0])
            for i, (t0, n) in enumerate(tl):
                if i + 1 < len(tl):
                    load(i + 1, tl[i + 1][0], tl[i + 1][1], tl[i + 1][0])
                do_tile(i, t0, n)
            S.barrier()

        S.barrier()
        stages = STAGES
        if stages >= 1:
            s5_phase(0, xT, hA, 0)
        if stages >= 2:
            mlp_phase(0, hA, 0, WIN)
        if stages >= 3:
            ple_phase(0, hA, 0, WIN, p0T)
        if stages >= 4:
            s5_phase(1, hA, hB, PRE)
        if stages >= 5:
            mlp_phase(1, hB, 0, OWNH)
        if stages >= 6:
            ple_phase(1, hB, 0, OWNH, pLT[0], do_kv=True)
        if stages >= 7:
            attn_phase(2)
        if stages >= 8:
            mlp_phase(2, hB, HALO, OWN)
        if stages >= 9:
            ple_phase(2, hB, HALO, OWN, pLT[1][:, HALO:OWNH])
        if stages >= 10:
            attn_phase(3)
        if stages >= 11:
            mlp_phase(3, hB, HALO, OWN)
        if stages >= 12:
            ple_phase(3, hB, HALO, OWN, pLT[2][:, HALO:OWNH], final=True)
        if stages < 12:
            srcb = hA[:, WIN - OWN:WIN] if stages <= 3 else hB[:, HALO:OWNH]
            S.add("sp", I("dma_start", out=outT, in_=srcb), writes=("dbg",), chan="dbg")
        S.finish()
    _DBG_MAPS[(WIN, OWN)] = dbg_map
    return nc


STAGES = 12
ATT_LEVEL = 6
DBGW = 0
PREP_ONLY = False
_DBG_MAPS = {}


def _rope_tables(pos):
    inv = ROPE_THETA ** (-np.arange(0, ROT, 2, dtype=np.float32) / ROT)
    ang = pos.astype(np.float32)[None, :] * inv[:, None].astype(np.float32)
    cos = np.cos(ang).astype(np.float32)
    sin = np.sin(ang).astype(np.float32)
    n = pos.shape[0]
    C = np.ones((64, n), np.float32)
    Sg = np.zeros((64, n), np.float32)
    C[0:8] = cos; C[8:16] = cos
    Sg[0:8] = -sin; Sg[8:16] = sin
    return np.concatenate([C, C], 0), np.concatenate([Sg, Sg], 0)


def _swap_cols(w, nheads):
    w = w.reshape(w.shape[0], nheads, HD)
    ws = w.copy()
    ws[:, :, 0:8] = w[:, :, 8:16]
    ws[:, :, 8:16] = w[:, :, 0:8]
    return np.ascontiguousarray(ws.reshape(w.shape[0], nheads * HD))


def _masks():
    qi = np.arange(128)[:, None] + 128
    kj = np.arange(256)[None, :]
    band = (kj <= qi) & (qi - kj < 128)
    mg = np.where(band, 0.0, -30000.0).astype(np.float32)
    m0 = np.where(band & (kj >= 128), 0.0, -30000.0).astype(np.float32)
    return m0, mg


_PROG_CACHE = {}


def run_config(inputs, NQ, n_cores=8):
    x = np.asarray(inputs["x"], np.float32)
    p = np.asarray(inputs["p"], np.float32)
    Bsz, SEQ, _ = x.shape
    OWN = SEQ // NQ
    WIN = SEQ
    HALO = 128
    OWNH = OWN + HALO
    key = (WIN, OWN)
    if key not in _PROG_CACHE:
        _PROG_CACHE[key] = build_program(WIN, OWN, HALO)
    nc = _PROG_CACHE[key]
    m0, mg = _masks()
    shared = {}
    for name in ("norm_mix", "ssm_lambda_re", "ssm_lambda_im", "ssm_log_dt", "ssm_b_re", "ssm_b_im",
                 "ssm_c_re", "ssm_c_im", "ssm_d", "ssm_w_glu", "kv_norm", "w_k", "w_v", "w_q",
                 "attn_sinks", "w_o", "norm_mlp", "w_up", "w_down", "norm_ple", "w_ple_gate",
                 "w_ple_proj", "norm_final"):
        shared[name] = np.ascontiguousarray(np.asarray(inputs[name], np.float32))
    shared["w_ks"] = _swap_cols(shared["w_k"], NKV)
    shared["w_qs"] = np.stack([_swap_cols(shared["w_q"][i], NH) for i in range(2)])
    shared["ident"] = np.eye(128, dtype=np.float32)
    bd = np.zeros((128, 128), np.float32)
    for g in range(8):
        bd[16 * g:16 * g + 16, 16 * g:16 * g + 16] = 1.0
    shared["bdmask"] = bd
    shared["maskG"] = mg
    in_maps = []
    cores = []
    for c in range(n_cores):
        cc = c % (Bsz * NQ)
        b, q = cc // NQ, cc % NQ
        cores.append((b, q))
        own_end = (q + 1) * OWN
        w0 = own_end - WIN
        xw = np.zeros((WIN, D), np.float32)
        pw = np.zeros((WIN, PLE), np.float32)
        lo = max(w0, 0)
        xw[lo - w0:] = x[b, lo:own_end]
        pw[lo - w0:] = p[0, b, lo:own_end]
        h0 = q * OWN - HALO
        pl = np.zeros((3, OWNH, PLE), np.float32)
        lo2 = max(h0, 0)
        pl[:, lo2 - h0:] = p[1:4, b, lo2:own_end]
        pos = np.arange(h0, own_end)
        cosT, sinT = _rope_tables(pos)
        m = dict(shared)
        m["xT"] = np.ascontiguousarray(xw.T)
        m["p0T"] = np.ascontiguousarray(pw.T)
        m["pLT"] = np.ascontiguousarray(pl.transpose(0, 2, 1))
        m["cosT"] = cosT
        m["sinT"] = sinT
        m["mask0"] = m0 if q == 0 else mg
        in_maps.append(m)
    if PREP_ONLY:
        return nc, in_maps
    res = run_bass_kernel_spmd(nc, in_maps, core_ids=list(range(n_cores)))
    if DBGW:
        global LAST_DBG
        LAST_DBG = ([r["dbg"] for r in res.results], _DBG_MAPS[key])
    out = np.zeros((Bsz, SEQ, D), np.float32)
    for c in range(min(n_cores, Bsz * NQ)):
        b, q = cores[c]
        out[b, q * OWN:(q + 1) * OWN] = res.results[c]["outT"].T
    return out


def kernel(**inputs):
    return run_config(inputs, NQ=4, n_cores=8)
```
